# Optimizing a Trainium2 kernel written in Bass

```python
import math
import jax, jax.numpy as jnp
from jax import lax
import numpy as np


D_MODEL = 1024
BATCH = 32
SEQ = 2048
DEPTH = 1

SSD_HEAD_DIM = 64
SSD_WIDTH = D_MODEL
SSD_HEADS = SSD_WIDTH // SSD_HEAD_DIM
SSD_GROUPS = 4
SSD_STATE = 128
SSD_CONV = 5
SSD_CHUNK = 128
XBC_WIDTH = SSD_WIDTH + 2 * SSD_GROUPS * SSD_STATE
S5_WIDTH = D_MODEL // 2
S5_GROUP_CH = 16
S5_GROUPS = S5_WIDTH // S5_GROUP_CH
S5_STATE = 64
MIX_WIDTH = SSD_WIDTH + S5_WIDTH
IN_PROJ_WIDTH = SSD_WIDTH + XBC_WIDTH + 2 * SSD_HEADS + S5_WIDTH
D_FF = 256 * ((8 * D_MODEL // 3 + 255) // 256)
FFN_CONV = 3
EPS = 1e-6

kernel_name = 'hybrid_ssd_s5_encoder_block'


def rms_norm(x, w):
    xf = x.astype(jnp.float32)
    xf = xf * lax.rsqrt(jnp.mean(xf * xf, axis=-1, keepdims=True) + EPS)
    return (xf * w.astype(jnp.float32)).astype(x.dtype)


def dwconv(x, w, b):
    k = w.shape[0]
    y = lax.conv_general_dilated(
        x, w[:, None, :].astype(x.dtype), window_strides=(1,),
        padding=[(k // 2, k // 2)], dimension_numbers=('NWC', 'WIO', 'NWC'),
        feature_group_count=x.shape[-1])
    return y + b.astype(x.dtype)


def ssd_scan(xs, dt, a_log, bm, cm):
    bsz, l, h, p = xs.shape
    g, n = bm.shape[2], bm.shape[3]
    r = h // g
    nc = l // SSD_CHUNK
    a = -jnp.exp(a_log.astype(jnp.float32))
    xdt = (xs.astype(jnp.float32) * dt[..., None]).reshape(bsz, nc, SSD_CHUNK, g, r, p)
    bc = bm.astype(jnp.float32).reshape(bsz, nc, SSD_CHUNK, g, n)
    cc = cm.astype(jnp.float32).reshape(bsz, nc, SSD_CHUNK, g, n)
    a_cs = jnp.cumsum((dt * a).reshape(bsz, nc, SSD_CHUNK, g, r), axis=2)
    diff = a_cs[:, :, :, None] - a_cs[:, :, None]
    mask = jnp.tril(jnp.ones((SSD_CHUNK, SSD_CHUNK), dtype=bool))[None, None, :, :, None, None]
    seg = jnp.exp(jnp.where(mask, diff, -jnp.inf))
    scores = jnp.einsum('bcqgn,bcsgn->bcqsg', cc, bc)
    y_diag = jnp.einsum('bcqsgr,bcsgrp->bcqgrp', scores[..., None] * seg, xdt)
    decay_states = jnp.exp(a_cs[:, :, -1:] - a_cs)
    states = jnp.einsum('bcsgn,bcsgr,bcsgrp->bcgrpn', bc, decay_states, xdt)
    chunk_decay = jnp.exp(a_cs[:, :, -1])

    def step(carry, inp):
        dec, st = inp
        return dec[..., None, None] * carry + st, carry

    init = jnp.zeros((bsz, g, r, p, n), jnp.float32)
    _, prev = lax.scan(step, init, (jnp.moveaxis(chunk_decay, 1, 0), jnp.moveaxis(states, 1, 0)))
    prev = jnp.moveaxis(prev, 0, 1)
    y_off = jnp.einsum('bcqgn,bcgrpn,bcqgr->bcqgrp', cc, prev, jnp.exp(a_cs))
    return (y_diag + y_off).reshape(bsz, l, h, p)


def ssd_mixer(z, xbc, dt_raw, conv_w, conv_b, dt_bias_f, dt_bias_b, a_log_f, a_log_b, d, norm_w):
    bsz, l, _ = z.shape
    xbc = jax.nn.silu(dwconv(xbc, conv_w, conv_b))
    xs, bm, cm = jnp.split(xbc, [SSD_WIDTH, SSD_WIDTH + SSD_GROUPS * SSD_STATE], axis=-1)
    xs = xs.reshape(bsz, l, SSD_HEADS, SSD_HEAD_DIM)
    bm = bm.reshape(bsz, l, SSD_GROUPS, SSD_STATE)
    cm = cm.reshape(bsz, l, SSD_GROUPS, SSD_STATE)
    dt_raw = dt_raw.astype(jnp.float32)
    dtf = jax.nn.softplus(dt_raw[..., :SSD_HEADS] + dt_bias_f.astype(jnp.float32))
    dtb = jax.nn.softplus(dt_raw[..., SSD_HEADS:] + dt_bias_b.astype(jnp.float32))
    y_f = ssd_scan(xs, dtf, a_log_f, bm, cm)
    y_b = jnp.flip(ssd_scan(jnp.flip(xs, 1), jnp.flip(dtb, 1), a_log_b,
                            jnp.flip(bm, 1), jnp.flip(cm, 1)), 1)
    y = y_f + y_b + d.astype(jnp.float32)[:, None] * xs.astype(jnp.float32)
    y = y.reshape(bsz, l, SSD_WIDTH) * jax.nn.silu(z.astype(jnp.float32))
    return rms_norm(y, norm_w).astype(z.dtype)


def _complex_combine(left, right):
    a1r, a1i, b1r, b1i = left
    a2r, a2i, b2r, b2i = right
    return (a2r * a1r - a2i * a1i,
            a2r * a1i + a2i * a1r,
            a2r * b1r - a2i * b1i + b2r,
            a2r * b1i + a2i * b1r + b2i)


def s5_scan(u, lam_re, lam_im, log_step, b_re, b_im, c_re, c_im, reverse):
    lam_re = lam_re.astype(jnp.float32)
    lam_im = lam_im.astype(jnp.float32)
    step = jnp.exp(log_step.astype(jnp.float32))[:, None]
    mag = jnp.exp(lam_re * step)
    ar = mag * jnp.cos(lam_im * step)
    ai = mag * jnp.sin(lam_im * step)
    den = lam_re * lam_re + lam_im * lam_im
    cr = ((ar - 1.0) * lam_re + ai * lam_im) / den
    ci = (ai * lam_re - (ar - 1.0) * lam_im) / den
    b_re = b_re.astype(jnp.float32)
    b_im = b_im.astype(jnp.float32)
    bbr = cr[..., None] * b_re - ci[..., None] * b_im
    bbi = cr[..., None] * b_im + ci[..., None] * b_re
    bur = jnp.einsum('blgc,gpc->lbgp', u, bbr)
    bui = jnp.einsum('blgc,gpc->lbgp', u, bbi)
    seq_len = u.shape[1]
    a_r = jnp.broadcast_to(ar[None, None], (seq_len, 1) + ar.shape)
    a_i = jnp.broadcast_to(ai[None, None], (seq_len, 1) + ai.shape)
    _, _, hr, hi = lax.associative_scan(_complex_combine, (a_r, a_i, bur, bui),
                                        reverse=reverse, axis=0)
    return (jnp.einsum('lbgp,gcp->blgc', hr, c_re.astype(jnp.float32))
            - jnp.einsum('lbgp,gcp->blgc', hi, c_im.astype(jnp.float32)))


def s5_mixer(u, lam_re_f, lam_im_f, log_step_f, lam_re_b, lam_im_b, log_step_b,
             b_re, b_im, c_re_f, c_im_f, c_re_b, c_im_b, d, glu_w, glu_b, norm_w):
    bsz, l, _ = u.shape
    uf = u.astype(jnp.float32).reshape(bsz, l, S5_GROUPS, S5_GROUP_CH)
    y = (s5_scan(uf, lam_re_f, lam_im_f, log_step_f, b_re, b_im, c_re_f, c_im_f, False)
         + s5_scan(uf, lam_re_b, lam_im_b, log_step_b, b_re, b_im, c_re_b, c_im_b, True)
         + d.astype(jnp.float32).reshape(S5_GROUPS, S5_GROUP_CH) * uf)
    g = jax.nn.gelu(y)
    gl = jnp.einsum('blgc,gcd->blgd', g, glu_w.astype(jnp.float32)) + glu_b.astype(jnp.float32)
    out = gl[..., :S5_GROUP_CH] * jax.nn.sigmoid(gl[..., S5_GROUP_CH:])
    return rms_norm(out.reshape(bsz, l, S5_WIDTH), norm_w).astype(u.dtype)


def conv_ffn(h, w_up, conv_w, conv_b, w_down):
    up = dwconv(h @ w_up, conv_w, conv_b)
    val, gate = jnp.split(up, 2, axis=-1)
    return (jax.nn.silu(gate) * val) @ w_down


def setup_inputs(seed: int = 0) -> dict:
    key = jax.random.key(seed)
    ks = iter(jax.random.split(key, 48))
    f32 = jnp.float32

    def nrm(shape, scale):
        return scale * jax.random.normal(next(ks), shape, f32)

    def unif(shape, lo, hi):
        return jax.random.uniform(next(ks), shape, f32, minval=lo, maxval=hi)

    x = jax.random.normal(next(ks), (BATCH, SEQ, D_MODEL), f32)
    norm_mix_w = 1.0 + nrm((DEPTH, D_MODEL), 0.02)
    w_in = nrm((DEPTH, D_MODEL, IN_PROJ_WIDTH), D_MODEL ** -0.5)
    ssd_conv_w = nrm((DEPTH, SSD_CONV, XBC_WIDTH), SSD_CONV ** -0.5)
    ssd_conv_b = nrm((DEPTH, XBC_WIDTH), 0.02)
    dt0_f = jnp.exp(unif((DEPTH, SSD_HEADS), math.log(1e-3), math.log(1e-1)))
    ssd_dt_bias_fwd = dt0_f + jnp.log(-jnp.expm1(-dt0_f))
    dt0_b = jnp.exp(unif((DEPTH, SSD_HEADS), math.log(1e-3), math.log(1e-1)))
    ssd_dt_bias_bwd = dt0_b + jnp.log(-jnp.expm1(-dt0_b))
    ssd_a_log_fwd = jnp.log(unif((DEPTH, SSD_HEADS), 1.0, 16.0))
    ssd_a_log_bwd = jnp.log(unif((DEPTH, SSD_HEADS), 1.0, 16.0))
    ssd_d = 1.0 + nrm((DEPTH, SSD_HEADS), 0.1)
    ssd_norm_w = 1.0 + nrm((DEPTH, SSD_WIDTH), 0.02)
    n_idx = math.pi * jnp.arange(S5_STATE, dtype=f32)
    s5_lambda_re_fwd = -0.5 + nrm((DEPTH, S5_GROUPS, S5_STATE), 0.01)
    s5_lambda_im_fwd = n_idx + nrm((DEPTH, S5_GROUPS, S5_STATE), 0.01)
    s5_log_step_fwd = unif((DEPTH, S5_GROUPS), math.log(1e-3), math.log(1e-1))
    s5_lambda_re_bwd = -0.5 + nrm((DEPTH, S5_GROUPS, S5_STATE), 0.01)
    s5_lambda_im_bwd = n_idx + nrm((DEPTH, S5_GROUPS, S5_STATE), 0.01)
    s5_log_step_bwd = unif((DEPTH, S5_GROUPS), math.log(1e-3), math.log(1e-1))
    s5_b_re = nrm((DEPTH, S5_GROUPS, S5_STATE, S5_GROUP_CH), (2 * S5_GROUP_CH) ** -0.5)
    s5_b_im = nrm((DEPTH, S5_GROUPS, S5_STATE, S5_GROUP_CH), (2 * S5_GROUP_CH) ** -0.5)
    s5_c_re_fwd = nrm((DEPTH, S5_GROUPS, S5_GROUP_CH, S5_STATE), (2 * S5_STATE) ** -0.5)
    s5_c_im_fwd = nrm((DEPTH, S5_GROUPS, S5_GROUP_CH, S5_STATE), (2 * S5_STATE) ** -0.5)
    s5_c_re_bwd = nrm((DEPTH, S5_GROUPS, S5_GROUP_CH, S5_STATE), (2 * S5_STATE) ** -0.5)
    s5_c_im_bwd = nrm((DEPTH, S5_GROUPS, S5_GROUP_CH, S5_STATE), (2 * S5_STATE) ** -0.5)
    s5_d = nrm((DEPTH, S5_WIDTH), 0.5)
    s5_glu_w = nrm((DEPTH, S5_GROUPS, S5_GROUP_CH, 2 * S5_GROUP_CH), S5_GROUP_CH ** -0.5)
    s5_glu_b = nrm((DEPTH, S5_GROUPS, 2 * S5_GROUP_CH), 0.02)
    s5_norm_w = 1.0 + nrm((DEPTH, S5_WIDTH), 0.02)
    w_out = nrm((DEPTH, MIX_WIDTH, D_MODEL), MIX_WIDTH ** -0.5)
    norm_ffn_w = 1.0 + nrm((DEPTH, D_MODEL), 0.02)
    ffn_w_up = nrm((DEPTH, D_MODEL, 2 * D_FF), D_MODEL ** -0.5)
    ffn_conv_w = nrm((DEPTH, FFN_CONV, 2 * D_FF), FFN_CONV ** -0.5)
    ffn_conv_b = nrm((DEPTH, 2 * D_FF), 0.02)
    ffn_w_down = nrm((DEPTH, D_FF, D_MODEL), D_FF ** -0.5)
    norm_final_w = 1.0 + nrm((D_MODEL,), 0.02)
    return {
        'x': x, 'norm_mix_w': norm_mix_w, 'w_in': w_in,
        'ssd_conv_w': ssd_conv_w, 'ssd_conv_b': ssd_conv_b,
        'ssd_dt_bias_fwd': ssd_dt_bias_fwd, 'ssd_dt_bias_bwd': ssd_dt_bias_bwd,
        'ssd_a_log_fwd': ssd_a_log_fwd, 'ssd_a_log_bwd': ssd_a_log_bwd,
        'ssd_d': ssd_d, 'ssd_norm_w': ssd_norm_w,
        's5_lambda_re_fwd': s5_lambda_re_fwd, 's5_lambda_im_fwd': s5_lambda_im_fwd,
        's5_log_step_fwd': s5_log_step_fwd,
        's5_lambda_re_bwd': s5_lambda_re_bwd, 's5_lambda_im_bwd': s5_lambda_im_bwd,
        's5_log_step_bwd': s5_log_step_bwd,
        's5_b_re': s5_b_re, 's5_b_im': s5_b_im,
        's5_c_re_fwd': s5_c_re_fwd, 's5_c_im_fwd': s5_c_im_fwd,
        's5_c_re_bwd': s5_c_re_bwd, 's5_c_im_bwd': s5_c_im_bwd,
        's5_d': s5_d, 's5_glu_w': s5_glu_w, 's5_glu_b': s5_glu_b, 's5_norm_w': s5_norm_w,
        'w_out': w_out, 'norm_ffn_w': norm_ffn_w,
        'ffn_w_up': ffn_w_up, 'ffn_conv_w': ffn_conv_w, 'ffn_conv_b': ffn_conv_b,
        'ffn_w_down': ffn_w_down, 'norm_final_w': norm_final_w,
    }


def reference(x, norm_mix_w, w_in, ssd_conv_w, ssd_conv_b, ssd_dt_bias_fwd, ssd_dt_bias_bwd,
              ssd_a_log_fwd, ssd_a_log_bwd, ssd_d, ssd_norm_w,
              s5_lambda_re_fwd, s5_lambda_im_fwd, s5_log_step_fwd,
              s5_lambda_re_bwd, s5_lambda_im_bwd, s5_log_step_bwd,
              s5_b_re, s5_b_im, s5_c_re_fwd, s5_c_im_fwd, s5_c_re_bwd, s5_c_im_bwd,
              s5_d, s5_glu_w, s5_glu_b, s5_norm_w, w_out, norm_ffn_w,
              ffn_w_up, ffn_conv_w, ffn_conv_b, ffn_w_down, norm_final_w):
    h = x
    cuts = [SSD_WIDTH, SSD_WIDTH + XBC_WIDTH, SSD_WIDTH + XBC_WIDTH + 2 * SSD_HEADS]
    for i in range(DEPTH):
        hn = rms_norm(h, norm_mix_w[i])
        proj = hn @ w_in[i]
        z, xbc, dt_raw, u = jnp.split(proj, cuts, axis=-1)
        y_ssd = ssd_mixer(z, xbc, dt_raw, ssd_conv_w[i], ssd_conv_b[i],
                          ssd_dt_bias_fwd[i], ssd_dt_bias_bwd[i],
                          ssd_a_log_fwd[i], ssd_a_log_bwd[i], ssd_d[i], ssd_norm_w[i])
        y_s5 = s5_mixer(u, s5_lambda_re_fwd[i], s5_lambda_im_fwd[i], s5_log_step_fwd[i],
                        s5_lambda_re_bwd[i], s5_lambda_im_bwd[i], s5_log_step_bwd[i],
                        s5_b_re[i], s5_b_im[i], s5_c_re_fwd[i], s5_c_im_fwd[i],
                        s5_c_re_bwd[i], s5_c_im_bwd[i], s5_d[i], s5_glu_w[i], s5_glu_b[i],
                        s5_norm_w[i])
        h = h + jnp.concatenate([y_ssd, y_s5], axis=-1) @ w_out[i]
        h = h + conv_ffn(rms_norm(h, norm_ffn_w[i]), ffn_w_up[i], ffn_conv_w[i],
                         ffn_conv_b[i], ffn_w_down[i])
    return rms_norm(h, norm_final_w)
```

```python
import math
import numpy as np
import ml_dtypes
from contextlib import ExitStack
import concourse.bass as bass
import concourse.mybir as mybir
from concourse.bass_utils import run_bass_kernel_spmd

F32 = mybir.dt.float32
BF16 = mybir.dt.bfloat16
AF = mybir.ActivationFunctionType
ALU = mybir.AluOpType

NCORES = 8
L = 2048
D = 1024
NCH = 16
DIN = 3616
DFF = 2816
EPS = 1e-6
MAGIC = 12582912.0
TWO_PI = 2.0 * math.pi
EPOCH = 3000


class Buf:
    __slots__ = ("name", "writer", "readers")

    def __init__(self, name):
        self.name = name
        self.writer = None
        self.readers = []


class K:
    def __init__(self, nc, es):
        self.nc = nc
        self.es = es
        self.eng = {"pe": nc.tensor, "act": nc.scalar, "dve": nc.vector, "pool": nc.gpsimd, "sp": nc.sync}
        self.sems = {e: [] for e in self.eng}
        self.count = {e: 0 for e in self.eng}
        self.seen = {e: {} for e in self.eng}
        self.ndma = 24
        self.dsem = [es.enter_context(nc.semaphore("dsem%d" % i)) for i in range(self.ndma)]
        self.dval = [0] * self.ndma
        self.dnext = 0
        self.all_tokens = []

    def _sem(self, e, epoch):
        while len(self.sems[e]) <= epoch:
            self.sems[e].append(self.es.enter_context(self.nc.semaphore("s_%s_%d" % (e, len(self.sems[e])))))
        return self.sems[e][epoch]

    def _wait(self, e, tok):
        kind = tok[0]
        if kind == "eng":
            _, src, idx = tok
            if src == e and e == "pe":
                return
            if self.seen[e].get(("eng", src), -1) >= idx:
                return
            self.seen[e][("eng", src)] = idx
            self.eng[e].wait_ge(self._sem(src, idx // EPOCH), (idx % EPOCH) + 1)
        else:
            _, si, val = tok
            if self.seen[e].get(("dma", si), -1) >= val:
                return
            self.seen[e][("dma", si)] = val
            self.eng[e].wait_ge(self.dsem[si], val)

    def _deps(self, e, reads, writes):
        toks = []
        for b in reads:
            if b.writer is not None:
                toks.append(b.writer)
        for b in writes:
            if b.writer is not None:
                toks.append(b.writer)
            for r in b.readers:
                if not (r[0] == "eng" and r[1] == e):
                    toks.append(r)
        for t in toks:
            self._wait(e, t)

    def _commit(self, tok, reads, writes):
        for b in writes:
            b.writer = tok
            b.readers = []
        for b in reads:
            b.readers.append(tok)
            if len(b.readers) > 24:
                b.readers = b.readers[-24:]

    def op(self, e, fn, reads=(), writes=()):
        self._deps(e, reads, writes)
        ins = fn()
        idx = self.count[e]
        self.count[e] += 1
        ins.then_inc(self._sem(e, idx // EPOCH), 1)
        tok = ("eng", e, idx)
        self._commit(tok, reads, writes)
        return tok

    def dma(self, out, in_, reads=(), writes=(), q="sp", **kw):
        self._deps(q, reads, writes)
        si = self.dnext
        self.dnext = (self.dnext + 1) % self.ndma
        if self.dval[si] > 0:
            self._wait(q, ("dma", si, self.dval[si]))
        self.dval[si] += 16
        self.eng[q].dma_start(out=out, in_=in_, **kw).then_inc(self.dsem[si], 16)
        tok = ("dma", si, self.dval[si])
        self._commit(tok, reads, writes)
        self.all_tokens.append(tok)
        if len(self.all_tokens) > 64:
            self.all_tokens = self.all_tokens[-64:]
        return tok

    def barrier(self):
        toks = [("eng", s, self.count[s] - 1) for s in self.eng if self.count[s] > 0]
        toks += [("dma", i, self.dval[i]) for i in range(self.ndma) if self.dval[i] > 0]
        for e in self.eng:
            for t in toks:
                if t[0] == "eng" and t[1] == e:
                    continue
                self._wait(e, t)

    def finish(self):
        for t in [("dma", i, self.dval[i]) for i in range(self.ndma) if self.dval[i] > 0]:
            self._wait("sp", t)
        for s in ("pe", "act", "dve", "pool"):
            if self.count[s] > 0:
                self._wait("sp", ("eng", s, self.count[s] - 1))


def bc(ap, shape):
    return ap.to_broadcast(list(shape))


def host_consts():
    c = {}
    c["c_ident"] = np.eye(128, dtype=np.float32)
    s = np.arange(128)[:, None]
    q = np.arange(128)[None, :]
    c["c_maskF"] = (s <= q).astype(np.float32)
    c["c_maskB"] = (s >= q).astype(np.float32)
    c["c_mle8"] = ((s // 16) <= (q // 16)).astype(np.float32)
    c["c_mge8"] = ((s // 16) >= (q // 16)).astype(np.float32)
    c["c_blk16"] = ((s // 16) == (q // 16)).astype(np.float32)
    sel = np.zeros((128, 4), np.float32)
    sel[:64, 0] = 1.0
    sel[64:, 1] = 1.0
    sel[64:, 2] = -1.0
    sel[:64, 3] = -1.0
    sel[64:, 3] = 1.0
    c["c_sel"] = sel
    es = np.zeros((96, 32, 128), np.float32)
    for r in range(96):
        es[r, r % 32, :] = 1.0
    c["c_esel"] = es.reshape(96, 32 * 128)
    p8 = np.arange(8, dtype=np.float32)
    ev = np.stack([7 - p8, p8 - 7, p8 + 1, p8, -p8, 8 - p8], 0)
    c["c_evec"] = np.broadcast_to(ev.reshape(1, 48), (128, 48)).copy()
    i16 = np.arange(16, dtype=np.float32)
    ev2 = np.stack([np.concatenate([8 * i16, [128.0]]), np.concatenate([8 * (15 - i16), [128.0]])], 0)
    c["c_evec2"] = np.broadcast_to(ev2.reshape(1, 34), (128, 34)).copy().astype(np.float32)
    return c


def build_program(S, dbg=None, stop_after=None):
    nc = bass.Bass("TRN2", target_bir_lowering=False)
    es = ExitStack()

    def din(name, shape, dt=F32):
        return nc.dram_tensor(name, list(shape), dt, kind="ExternalInput").ap()

    x = din("x", [S, L, D])
    out = nc.dram_tensor("out", [S, L, D], F32, kind="ExternalOutput").ap()
    P = {}
    shapes = dict(
        norm_mix_w=[D], w_in=[D, DIN], ssd_conv_w=[5, 2048], ssd_conv_b=[2048],
        ssd_dt_bias_fwd=[16], ssd_dt_bias_bwd=[16], ssd_a_log_fwd=[16], ssd_a_log_bwd=[16],
        ssd_d=[16], ssd_norm_w=[D],
        s5_lambda_re_fwd=[32, 64], s5_lambda_im_fwd=[32, 64], s5_log_step_fwd=[32],
        s5_lambda_re_bwd=[32, 64], s5_lambda_im_bwd=[32, 64], s5_log_step_bwd=[32],
        s5_b_re=[32, 64, 16], s5_b_im=[32, 64, 16],
        s5_c_re_fwd=[32, 16, 64], s5_c_im_fwd=[32, 16, 64], s5_c_re_bwd=[32, 16, 64], s5_c_im_bwd=[32, 16, 64],
        s5_d=[512], s5_glu_w=[32, 16, 32], s5_glu_b=[32, 32], s5_norm_w=[512],
        w_out=[1536, D], norm_ffn_w=[D], ffn_w_up=[D, 2 * DFF], ffn_conv_w=[3, 2 * DFF], ffn_conv_b=[2 * DFF],
        ffn_w_down=[DFF, D], norm_final_w=[D],
    )
    for n, sh in shapes.items():
        P[n] = din(n, sh)
    C = {n: din(n, list(v.shape)) for n, v in host_consts().items()}
    dbg_out = {}
    if dbg:
        for n, sh in dbg.items():
            dbg_out[n] = nc.dram_tensor(n, list(sh), F32, kind="ExternalOutput").ap()

    def dscr(name, shape, dt=BF16):
        return nc.dram_tensor(name, list(shape), dt).ap()

    win_b = dscr("win_b", [D, DIN])
    wout_b = dscr("wout_b", [1536, D])
    wup_b = dscr("wup_b", [D, 2 * DFF])
    wdn_b = dscr("wdn_b", [DFF, D])
    hnfm_d = dscr("hnfm_d", [D, L])
    xs_d = dscr("xs_d", [L, D])
    btm_d = dscr("btm_d", [L, 512])
    bfm_d = dscr("bfm_d", [512, L])
    cfm_d = dscr("cfm_d", [512, L])
    zs_d = dscr("zs_d", [L, D])
    h1_d = dscr("h1_d", [L, D], F32)
    hn2fm_d = dscr("hn2fm_d", [D, L])
    g_d = dscr("g_d", [DFF, L])
    S5W = 32 * 7 * 128
    s5w_d = dscr("s5w_d", [128, S5W])
    us_d = dscr("us_d", [512, L])
    ys_d = dscr("ys_d", [512, L])

    with es:
        k = K(nc, es)
        sb = lambda name, shape, dt=F32: es.enter_context(nc.sbuf_tensor(name, list(shape), dt))
        ident_f = sb("ident_f", [128, 128])
        ident_b = sb("ident_b", [128, 128], BF16)
        maskF_b = sb("maskF_b", [128, 128], BF16)
        maskB_b = sb("maskB_b", [128, 128], BF16)
        maskF_f = sb("maskF_f", [128, 128])
        maskB_f = sb("maskB_f", [128, 128])
        ones_b = sb("ones_b", [128, 128], BF16)
        esel_b = sb("esel_b", [96, 32 * 128], BF16)
        nmix_fm = sb("nmix_fm", [128, 8])
        nssd_fm = sb("nssd_fm", [128, 8])
        nffn_fm = sb("nffn_fm", [128, 8])
        ns5_fm = sb("ns5_fm", [128, 4])
        nfin_bc = sb("nfin_bc", [128, D])
        cw_ssd = sb("cw_ssd", [128, 16, 5])
        cb_ssd = sb("cb_ssd", [128, 16])
        cw_ffn = sb("cw_ffn", [128, 44, 3])
        cb_ffn = sb("cb_ffn", [128, 44])
        a_bc = sb("a_bc", [128, 32])
        dtb_bc = sb("dtb_bc", [128, 32])
        dsk_bc = sb("dsk_bc", [128, 16])
        eps_t = sb("eps_t", [128, 1])
        one_t = sb("one_t", [128, 1])
        ys5_fm = sb("ys5_fm", [128, 4, L], BF16)
        ARENA = 164 * 1024
        arena = sb("arena", [128, ARENA // 2], BF16)
        pst = [es.enter_context(nc.psum_tensor("ps%d" % i, [128, 512], F32)) for i in range(8)]
        PB = [Buf("psb%d" % i) for i in range(8)]

        st = {"off": 0}

        def arena_reset(keep=0):
            k.barrier()
            st["off"] = keep

        def carve(shape, dt=F32):
            n = 1
            for d_ in shape[1:]:
                n *= d_
            nbytes = n * (4 if dt == F32 else 2)
            nbytes = (nbytes + 63) // 64 * 64
            o = st["off"]
            st["off"] += nbytes
            assert st["off"] <= ARENA, ("arena overflow", st["off"])
            v = arena[0:shape[0], o // 2:(o + nbytes) // 2]
            if dt == F32:
                v = v.bitcast(F32)[:, 0:n]
            else:
                v = v[:, 0:n]
            if len(shape) == 3:
                v = v.rearrange("p (a b) -> p a b", a=shape[1])
            elif len(shape) == 4:
                v = v.rearrange("p (a b c) -> p a b c", a=shape[1], b=shape[2])
            return v

        V, A_, G_, T_ = nc.vector, nc.scalar, nc.gpsimd, nc.tensor

        B_const = Buf("consts")
        wb_bufs = {n: Buf(n) for n in ("win", "wout", "wup", "wdn")}
        for (dst, src, rows, bn) in ((win_b, P["w_in"], D, "win"), (wout_b, P["w_out"], 1536, "wout"),
                                     (wup_b, P["ffn_w_up"], D, "wup"), (wdn_b, P["ffn_w_down"], DFF, "wdn")):
            step = 256
            for r0 in range(0, rows, step):
                k.dma(dst[r0:r0 + step, :], src[r0:r0 + step, :], writes=[wb_bufs[bn]], q="pool")

        def ld(dst_ap, src_ap, **kw):
            k.dma(dst_ap, src_ap, writes=[B_const], **kw)

        ld(ident_f[:], C["c_ident"])
        ld(maskF_f[:], C["c_maskF"])
        ld(maskB_f[:], C["c_maskB"])
        ld(nmix_fm[:], P["norm_mix_w"].rearrange("(k p) -> p k", p=128), allow_slow_non_contiguous=True)
        ld(nssd_fm[:], P["ssd_norm_w"].rearrange("(k p) -> p k", p=128), allow_slow_non_contiguous=True)
        ld(nffn_fm[:], P["norm_ffn_w"].rearrange("(k p) -> p k", p=128), allow_slow_non_contiguous=True)
        ld(ns5_fm[:], P["s5_norm_w"].rearrange("(k p) -> p k", p=128), allow_slow_non_contiguous=True)
        ld(nfin_bc[:], P["norm_final_w"].rearrange("(o d) -> o d", o=1).to_broadcast([128, D]))
        for t_ in range(5):
            ld(cw_ssd[:, :, t_], P["ssd_conv_w"][t_].rearrange("(k p) -> p k", p=128), allow_slow_non_contiguous=True)
        ld(cb_ssd[:], P["ssd_conv_b"].rearrange("(k p) -> p k", p=128), allow_slow_non_contiguous=True)
        for t_ in range(3):
            ld(cw_ffn[:, :, t_], P["ffn_conv_w"][t_].rearrange("(k p) -> p k", p=128), allow_slow_non_contiguous=True)
        ld(cb_ffn[:], P["ffn_conv_b"].rearrange("(k p) -> p k", p=128), allow_slow_non_contiguous=True)
        ld(a_bc[:, 0:16], P["ssd_a_log_fwd"].rearrange("(o d) -> o d", o=1).to_broadcast([128, 16]))
        ld(a_bc[:, 16:32], P["ssd_a_log_bwd"].rearrange("(o d) -> o d", o=1).to_broadcast([128, 16]))
        ld(dtb_bc[:, 0:16], P["ssd_dt_bias_fwd"].rearrange("(o d) -> o d", o=1).to_broadcast([128, 16]))
        ld(dtb_bc[:, 16:32], P["ssd_dt_bias_bwd"].rearrange("(o d) -> o d", o=1).to_broadcast([128, 16]))
        ld(dsk_bc[:], P["ssd_d"].rearrange("(o d) -> o d", o=1).to_broadcast([128, 16]))
        st["off"] = 0
        esel_f = carve([96, 32 * 128])
        ld(esel_f, C["c_esel"])
        k.op("dve", lambda: V.tensor_copy(out=esel_b[:], in_=esel_f), reads=[B_const], writes=[B_const])
        k.op("dve", lambda: V.tensor_copy(out=ident_b[:], in_=ident_f[:]), reads=[B_const], writes=[B_const])
        k.op("dve", lambda: V.tensor_copy(out=maskF_b[:], in_=maskF_f[:]), reads=[B_const], writes=[B_const])
        k.op("dve", lambda: V.tensor_copy(out=maskB_b[:], in_=maskB_f[:]), reads=[B_const], writes=[B_const])
        k.op("dve", lambda: V.memset(ones_b[:], 1.0), writes=[B_const])
        k.op("dve", lambda: V.memset(eps_t[:], EPS), writes=[B_const])
        k.op("dve", lambda: V.memset(one_t[:], 1.0), writes=[B_const])
        k.op("act", lambda: A_.activation(out=a_bc[:], in_=a_bc[:], func=AF.Exp), reads=[B_const], writes=[B_const])
        k.op("dve", lambda: V.tensor_scalar(out=a_bc[:], in0=a_bc[:], scalar1=-1.0, scalar2=None, op0=ALU.mult),
             reads=[B_const], writes=[B_const])

        B_s5w = Buf('s5w_d')
        B_ys5 = Buf('ys5')
        B_hnfm_d = Buf('hnfm_d')
        blk16_b = sb('blk16_b', [128, 128], BF16)
        g = NS()
        g.__dict__.update({kk: vv for kk, vv in locals().items() if kk != "g"})
        s5_prep(g)
        for b in range(S):
            run_sequence(g, b)
        k.finish()
    return nc


class NS:
    pass


def s5_prep(g):
    nc, k, P, C = g.nc, g.k, g.P, g.C
    V, A_, T_ = nc.vector, nc.scalar, nc.tensor
    carve, pst, PB, sb = g.carve, g.pst, g.PB, g.sb
    ident_f = g.ident_f
    g.arena_reset()
    sel = sb("sel", [128, 4])
    g.gbias = sb("gbias", [128, 32, 2])
    g.lam8r = sb("lam8r", [128, 2, 32])
    g.lam8i = sb("lam8i", [128, 2, 32])
    g.tabr = sb("tabr", [128, 2, 32, 17])
    g.tabi = sb("tabi", [128, 2, 32, 17])
    BP = Buf("s5prep")
    RW = dict(reads=[BP], writes=[BP])
    evec = carve([128, 48])
    evec2 = carve([128, 34])
    mle8 = carve([128, 128])
    mge8 = carve([128, 128])
    blk16 = carve([128, 128])
    Dp = carve([128, 32])
    glu_rep = carve([128, 32, 32])
    Br = carve([128, 32, 16])
    Bi = carve([128, 32, 16])
    for (t, n) in ((sel[:], "c_sel"), (evec, "c_evec"), (evec2, "c_evec2"), (mle8, "c_mle8"), (mge8, "c_mge8"), (blk16, "c_blk16")):
        k.dma(t, C[n], writes=[BP])
    for s in range(8):
        r = slice(16 * s, 16 * s + 16)
        k.dma(Dp[r, :], P["s5_d"].rearrange("(g c) -> c g", c=16), writes=[BP], allow_slow_non_contiguous=True)
        k.dma(glu_rep[r, :, :], P["s5_glu_w"].rearrange("g c d -> c g d"), writes=[BP])
        k.dma(g.gbias[r, :, :], P["s5_glu_b"].rearrange("g (h d) -> d g h", h=2), writes=[BP],
              allow_slow_non_contiguous=True)
    for hf in range(2):
        r = slice(64 * hf, 64 * hf + 64)
        k.dma(Br[r], P["s5_b_re"].rearrange("g p c -> p g c"), writes=[BP])
        k.dma(Bi[r], P["s5_b_im"].rearrange("g p c -> p g c"), writes=[BP])
    k.op("dve", lambda: V.tensor_scalar(out=g.gbias[:, :, 1], in0=g.gbias[:, :, 1], scalar1=0.5, scalar2=None,
                                        op0=ALU.mult), **RW)
    k.op("dve", lambda: V.tensor_copy(out=g.blk16_b[:], in_=blk16), reads=[BP], writes=[g.B_const])

    PW = []
    BB = []
    CC = []
    tA = carve([128, 32, 48])
    tB = carve([128, 32, 48])
    tC = carve([128, 32, 48])
    Cld = carve([128, 4, 64])
    for d, sfx in enumerate(("fwd", "bwd")):
        lamre = carve([128, 32])
        lamim = carve([128, 32])
        step = carve([128, 32])
        for hf in range(2):
            r = slice(64 * hf, 64 * hf + 64)
            k.dma(lamre[r], P["s5_lambda_re_" + sfx].rearrange("g p -> p g"), writes=[BP],
                  allow_slow_non_contiguous=True)
            k.dma(lamim[r], P["s5_lambda_im_" + sfx].rearrange("g p -> p g"), writes=[BP],
                  allow_slow_non_contiguous=True)
        k.dma(step, P["s5_log_step_" + sfx].rearrange("(o g) -> o g", o=1).to_broadcast([128, 32]), writes=[BP])
        Cr = carve([128, 32, 16])
        Ci = carve([128, 32, 16])
        for (dst, nm) in ((Cr, "s5_c_re_" + sfx), (Ci, "s5_c_im_" + sfx)):
            k.dma(Cld, P[nm].rearrange("(t gg) c p -> (gg c) t p", t=4), writes=[BP])
            dflat = dst.rearrange("p g c -> p (g c)")
            for t in range(4):
                k.op("pe", lambda t=t: T_.transpose(out=pst[0][0:64, t * 128:(t + 1) * 128], in_=Cld[:, t, :],
                                                    identity=ident_f[:]), reads=[BP, g.B_const], writes=[PB[0]])
            k.op("dve", lambda: V.tensor_copy(out=dflat[0:64, :], in_=pst[0][0:64, :]), reads=[PB[0]], writes=[BP])
            k.op("dve", lambda: V.tensor_copy(out=dflat[64:128, :], in_=pst[0][0:64, :]), reads=[PB[0]], writes=[BP])
        CC.append((Cr, Ci))
        k.op("act", lambda: A_.activation(out=step, in_=step, func=AF.Exp), **RW)
        sr = carve([128, 32])
        si = carve([128, 32])
        k.op("dve", lambda: V.tensor_mul(out=sr, in0=lamre, in1=step), **RW)
        k.op("dve", lambda: V.tensor_mul(out=si, in0=lamim, in1=step), **RW)
        PWr = carve([128, 32, 48])
        PWi = carve([128, 32, 48])

        def cpow(dR, dI, ev_ap, n):
            tA_, tB_, tC_ = tA[:, :, 0:n], tB[:, :, 0:n], tC[:, :, 0:n]
            ev_b = ev_ap.unsqueeze(1).to_broadcast([128, 32, n])
            k.op("dve", lambda: V.tensor_tensor(out=tA_, in0=sr.unsqueeze(2).to_broadcast([128, 32, n]), in1=ev_b,
                                                op=ALU.mult), **RW)
            k.op("act", lambda: A_.activation(out=tA_, in_=tA_, func=AF.Exp), **RW)
            k.op("dve", lambda: V.tensor_tensor(out=tB_, in0=si.unsqueeze(2).to_broadcast([128, 32, n]), in1=ev_b,
                                                op=ALU.mult), **RW)

            def sin_of(dst, shift):
                if shift != 0.0:
                    k.op("dve", lambda: V.tensor_scalar(out=dst, in0=tB_, scalar1=shift, scalar2=None, op0=ALU.add),
                         **RW)
                    src = dst
                else:
                    src = tB_
                k.op("dve", lambda: V.tensor_scalar(out=tC_, in0=src, scalar1=1.0 / TWO_PI, scalar2=MAGIC,
                                                    op0=ALU.mult, op1=ALU.add), **RW)
                k.op("dve", lambda: V.tensor_scalar(out=tC_, in0=tC_, scalar1=MAGIC, scalar2=-TWO_PI,
                                                    op0=ALU.subtract, op1=ALU.mult), **RW)
                k.op("dve", lambda: V.tensor_tensor(out=dst, in0=src, in1=tC_, op=ALU.add), **RW)
                k.op("act", lambda: A_.activation(out=dst, in_=dst, func=AF.Sin, scale=1.0 - 2e-6), **RW)

            sin_of(dI, 0.0)
            sin_of(dR, 0.5 * math.pi)
            k.op("dve", lambda: V.tensor_mul(out=dR, in0=dR, in1=tA_), **RW)
            k.op("dve", lambda: V.tensor_mul(out=dI, in0=dI, in1=tA_), **RW)

        cpow(PWr, PWi, evec, 48)
        cpow(g.tabr[:, d, :, :], g.tabi[:, d, :, :], evec2[:, 17 * d:17 * d + 17], 17)
        k.op("dve", lambda: V.tensor_scalar(out=g.tabi[:, d, :, :].rearrange("p g e -> p (g e)"),
                                            in0=g.tabi[:, d, :, :].rearrange("p g e -> p (g e)"), scalar1=sel[:, 3:4],
                                            scalar2=None, op0=ALU.mult), **RW)
        PW.append((PWr, PWi))
        k.op("dve", lambda: V.tensor_copy(out=g.lam8r[:, d, :], in_=PWr[:, :, 23]), reads=[BP], writes=[BP])
        k.op("dve", lambda: V.tensor_scalar(out=g.lam8i[:, d, :], in0=PWi[:, :, 23], scalar1=sel[:, 3:4], scalar2=None,
                                            op0=ALU.mult), **RW)
        ar = PWr[:, :, 16]
        ai = PWi[:, :, 16]
        den = carve([128, 32])
        am1 = carve([128, 32])
        cr = carve([128, 32])
        ci = carve([128, 32])
        t1 = carve([128, 32])
        k.op("dve", lambda: V.tensor_mul(out=den, in0=lamre, in1=lamre), **RW)
        k.op("dve", lambda: V.tensor_mul(out=t1, in0=lamim, in1=lamim), **RW)
        k.op("dve", lambda: V.tensor_add(out=den, in0=den, in1=t1), **RW)
        k.op("dve", lambda: V.reciprocal(out=den, in_=den), **RW)
        k.op("dve", lambda: V.tensor_scalar(out=am1, in0=ar, scalar1=-1.0, scalar2=None, op0=ALU.add), **RW)
        k.op("dve", lambda: V.tensor_mul(out=cr, in0=am1, in1=lamre), **RW)
        k.op("dve", lambda: V.tensor_mul(out=t1, in0=ai, in1=lamim), **RW)
        k.op("dve", lambda: V.tensor_add(out=cr, in0=cr, in1=t1), **RW)
        k.op("dve", lambda: V.tensor_mul(out=cr, in0=cr, in1=den), **RW)
        k.op("dve", lambda: V.tensor_mul(out=ci, in0=ai, in1=lamre), **RW)
        k.op("dve", lambda: V.tensor_mul(out=t1, in0=am1, in1=lamim), **RW)
        k.op("dve", lambda: V.tensor_sub(out=ci, in0=ci, in1=t1), **RW)
        k.op("dve", lambda: V.tensor_mul(out=ci, in0=ci, in1=den), **RW)
        Bbr = carve([128, 32, 16])
        Bbi = carve([128, 32, 16])
        t3 = carve([128, 32, 16])
        crb = cr.unsqueeze(2).to_broadcast([128, 32, 16])
        cib = ci.unsqueeze(2).to_broadcast([128, 32, 16])
        k.op("dve", lambda: V.tensor_tensor(out=Bbr, in0=Br, in1=crb, op=ALU.mult), **RW)
        k.op("dve", lambda: V.tensor_tensor(out=t3, in0=Bi, in1=cib, op=ALU.mult), **RW)
        k.op("dve", lambda: V.tensor_sub(out=Bbr, in0=Bbr, in1=t3), **RW)
        k.op("dve", lambda: V.tensor_tensor(out=Bbi, in0=Bi, in1=crb, op=ALU.mult), **RW)
        k.op("dve", lambda: V.tensor_tensor(out=t3, in0=Br, in1=cib, op=ALU.mult), **RW)
        k.op("dve", lambda: V.tensor_add(out=Bbi, in0=Bbi, in1=t3), **RW)
        BB.append((Bbr, Bbi))

    GH = 16
    RE = carve([128, GH, 8, 16])
    IM = carve([128, GH, 8, 16])
    T1 = carve([128, GH, 8, 16])
    TAB = [carve([128, GH, 8, 16]) for _ in range(4)]
    stage = carve([128, GH, 7, 128], BF16)
    tq = carve([128, 4, 128])
    B_stage = Buf("s5stage")

    def table(out, d, idx, Mr, Mi, selcol, g0):
        PWr, PWi = PW[d]
        pr = PWr[:, g0:g0 + GH, idx * 8:(idx + 1) * 8].unsqueeze(3).to_broadcast([128, GH, 8, 16])
        pi = PWi[:, g0:g0 + GH, idx * 8:(idx + 1) * 8].unsqueeze(3).to_broadcast([128, GH, 8, 16])
        mr = Mr[:, g0:g0 + GH, :].unsqueeze(2).to_broadcast([128, GH, 8, 16])
        mi = Mi[:, g0:g0 + GH, :].unsqueeze(2).to_broadcast([128, GH, 8, 16])
        k.op("dve", lambda: V.tensor_tensor(out=RE, in0=mr, in1=pr, op=ALU.mult), **RW)
        k.op("dve", lambda: V.tensor_tensor(out=T1, in0=mi, in1=pi, op=ALU.mult), **RW)
        k.op("dve", lambda: V.tensor_sub(out=RE, in0=RE, in1=T1), **RW)
        k.op("dve", lambda: V.tensor_tensor(out=IM, in0=mr, in1=pi, op=ALU.mult), **RW)
        k.op("dve", lambda: V.tensor_tensor(out=T1, in0=mi, in1=pr, op=ALU.mult), **RW)
        k.op("dve", lambda: V.tensor_add(out=IM, in0=IM, in1=T1), **RW)
        fl = lambda a: a.rearrange("p g s c -> p (g s c)")
        k.op("dve", lambda: V.tensor_scalar(out=fl(RE), in0=fl(RE), scalar1=sel[:, 0:1], scalar2=None, op0=ALU.mult),
             **RW)
        k.op("dve", lambda: V.scalar_tensor_tensor(out=fl(out), in0=fl(IM), scalar=sel[:, selcol:selcol + 1],
                                                   in1=fl(RE), op0=ALU.mult, op1=ALU.add),
             reads=[BP, B_stage], writes=[BP, B_stage])

    for g0 in (0, 16):
        for d in range(2):
            Bbr, Bbi = BB[d]
            Cr, Ci = CC[d]
            table(TAB[2 * d], d, 3 * d + 0, Bbr, Bbi, 1, g0)
            table(TAB[2 * d + 1], d, 3 * d + 1, Cr, Ci, 2, g0)
            table(T1, d, 3 * d + 2, Cr, Ci, 2, g0)
            k.op("dve", lambda d=d: V.tensor_copy(out=stage[:, :, 3 + d, :],
                                                  in_=T1.rearrange("p g s c -> p g (s c)")),
                 reads=[BP], writes=[B_stage])
        for gq in range(0, GH, 4):
            for d in range(2):
                for j in range(4):
                    k.op("pe", lambda d=d, j=j: T_.transpose(
                        out=pst[1 + d][:, j * 128:(j + 1) * 128],
                        in_=TAB[2 * d][:, gq + j].rearrange("p s c -> p (s c)"), identity=ident_f[:]),
                        reads=[BP, B_stage], writes=[PB[1 + d]])
                k.op("act", lambda d=d: A_.copy(out=stage[:, gq:gq + 4, 1 + d, :],
                                                in_=pst[1 + d][:, :].rearrange("p (g n) -> p g n", g=4)),
                     reads=[PB[1 + d]], writes=[B_stage])
            for d in range(2):
                for j in range(4):
                    k.op("pe", lambda d=d, j=j: T_.matmul(
                        pst[3 + d][:, j * 128:(j + 1) * 128],
                        lhsT=TAB[2 * d][:, gq + j].rearrange("p s c -> p (s c)"),
                        rhs=TAB[2 * d + 1][:, gq + j].rearrange("p s c -> p (s c)"), start=True, stop=True),
                        reads=[BP, B_stage], writes=[PB[3 + d]])
            k.op("dve", lambda: V.tensor_tensor(out=tq, in0=pst[3][:, :].rearrange("p (g n) -> p g n", g=4),
                                                in1=mle8.unsqueeze(1).to_broadcast([128, 4, 128]), op=ALU.mult),
                 reads=[PB[3], BP], writes=[BP])
            tq2 = RE.rearrange("p g s c -> p (g s c)")[:, 0:512].rearrange("p (g n) -> p g n", g=4)
            k.op("dve", lambda: V.tensor_tensor(out=tq2, in0=pst[4][:, :].rearrange("p (g n) -> p g n", g=4),
                                                in1=mge8.unsqueeze(1).to_broadcast([128, 4, 128]), op=ALU.mult),
                 reads=[PB[4], BP], writes=[BP])
            k.op("dve", lambda: V.tensor_add(out=tq, in0=tq, in1=tq2), **RW)
            for j in range(4):
                gg = g0 + gq + j
                k.op("dve", lambda j=j, gg=gg: V.scalar_tensor_tensor(
                    out=tq[:, j, :], in0=ident_f[:], scalar=Dp[:, gg:gg + 1], in1=tq[:, j, :],
                    op0=ALU.mult, op1=ALU.add), reads=[BP, g.B_const], writes=[BP])
            k.op("dve", lambda: V.tensor_copy(out=stage[:, gq:gq + 4, 0, :], in_=tq), reads=[BP], writes=[B_stage])
        for h in range(2):
            gsrc = glu_rep[:, g0:g0 + GH, h * 16:(h + 1) * 16].unsqueeze(2).to_broadcast([128, GH, 8, 16])
            bsrc = blk16.rearrange("p (q d) -> p q d", q=8).unsqueeze(1).to_broadcast([128, GH, 8, 16])
            k.op("dve", lambda: V.tensor_tensor(out=RE, in0=gsrc, in1=bsrc, op=ALU.mult), **RW)
            k.op("dve", lambda h=h: V.tensor_scalar(out=stage[:, :, 5 + h, :],
                                                    in0=RE.rearrange("p g s c -> p g (s c)"), scalar1=0.5,
                                                    scalar2=None, op0=ALU.mult), reads=[BP], writes=[B_stage])
        k.dma(g.s5w_d[:, g0 * 896:(g0 + GH) * 896], stage.rearrange("p g k n -> p (g k n)"), reads=[B_stage],
              writes=[g.B_s5w])


def pipeline(n, stages, name=""):
    import os
    if os.environ.get("NOPIPE") or (os.environ.get("PIPE") is not None and name not in os.environ["PIPE"].split(",")):
        for m in range(n):
            for sk, fn in sorted(stages, key=lambda z: z[0]):
                fn(m)
        return
    mx = max(sk for sk, _ in stages)
    for t in range(n + mx):
        for sk, fn in sorted(stages, key=lambda z: -z[0]):
            m = t - sk
            if 0 <= m < n:
                fn(m)


def dump(g, b, name, ap, reads):
    if b == 0 and name in g.dbg_out:
        g.k.dma(g.dbg_out[name], ap, reads=reads, writes=[Buf("dbg")])


def run_sequence(g, b):
    phase_a(g, b)
    if g.stop_after == "a":
        return
    phase_b(g, b)
    if g.stop_after == "b":
        return
    phase_d(g, b)


def phase_a(g, b):
    nc, k, P = g.nc, g.k, g.P
    V, A_, T_ = nc.vector, nc.scalar, nc.tensor
    carve, pst, PB = g.carve, g.pst, g.PB
    g.arena_reset()
    OGU = carve([128, 4, L], BF16)
    B_ogu = Buf("ogu")
    s5w = carve([128, 32, 7, 128], BF16)
    B_w = Buf("s5w")
    for h in range(2):
        k.dma(s5w[:, 16 * h:16 * h + 16].rearrange("p g k n -> p (g k n)"),
              g.s5w_d[:, 16 * h * 896:(16 * h + 16) * 896], reads=[g.B_s5w], writes=[B_w])
    hn_fm = carve([128, 8, L], BF16)
    B_hn = Buf("hn_fm")
    Wu = carve([128, 8, 512], BF16)
    B_Wu = Buf("Wu")
    xt = [carve([128, D]) for _ in range(2)]
    B_xt = [Buf("xt0"), Buf("xt1")]
    junk = carve([128, D], BF16)
    B_junk = Buf("junk")
    hnb = [carve([128, D], BF16) for _ in range(2)]
    B_hnb = [Buf("hnb0"), Buf("hnb1")]
    stat = [carve([128, 4]) for _ in range(2)]
    B_stat = [Buf("st0"), Buf("st1")]
    BC = g.B_const

    k.dma(Wu, g.win_b[:, 3104:3616].rearrange("(kt p) n -> p kt n", p=128), reads=[g.wb_bufs["win"]], writes=[B_Wu])
    def p0a(c):
        i = c % 2
        k.dma(xt[i], g.x[b, c * 128:(c + 1) * 128, :], writes=[B_xt[i]])
        k.op("act", lambda: A_.activation(out=junk, in_=xt[i], func=AF.Square, accum_out=stat[i][:, 0:1]),
             reads=[B_xt[i]], writes=[B_junk, B_stat[i]])
        k.op("act", lambda: A_.activation(out=stat[i][:, 1:2], in_=stat[i][:, 0:1], func=AF.Ln, scale=1.0 / D,
                                          bias=g.eps_t[:]), reads=[B_stat[i], BC], writes=[B_stat[i]])
        k.op("act", lambda: A_.activation(out=stat[i][:, 2:3], in_=stat[i][:, 1:2], func=AF.Exp, scale=-0.5),
             reads=[B_stat[i]], writes=[B_stat[i]])
        k.op("dve", lambda: V.tensor_scalar(out=hnb[i], in0=xt[i], scalar1=stat[i][:, 2:3], scalar2=None,
                                            op0=ALU.mult), reads=[B_xt[i], B_stat[i]], writes=[B_hnb[i]])

    def p0b(c):
        i = c % 2
        psb = pst[i][:, :].bitcast(BF16)

        def tr():
            ins = None
            for kt in range(8):
                ins = T_.transpose(out=psb[:, kt * 128:(kt + 1) * 128], in_=hnb[i][:, kt * 128:(kt + 1) * 128],
                                   identity=g.ident_b[:])
            return ins
        k.op("pe", tr, reads=[B_hnb[i], BC], writes=[PB[i]])
        k.op("dve", lambda: V.tensor_tensor(out=hn_fm[:, :, c * 128:(c + 1) * 128],
                                            in0=psb.rearrange("p (k t) -> p k t", k=8),
                                            in1=g.nmix_fm[:].unsqueeze(2).to_broadcast([128, 8, 128]), op=ALU.mult),
             reads=[PB[i], BC], writes=[B_hn])
        k.dma(g.hnfm_d.rearrange("(kt p) t -> p kt t", p=128)[:, :, c * 128:(c + 1) * 128],
              hn_fm[:, :, c * 128:(c + 1) * 128], reads=[B_hn], writes=[g.B_hnfm_d])

    pipeline(NCH, [(0, p0a), (1, p0b)], "p0")
    if b == 0 and "d_hn" in g.dbg_out:
        dtmp = carve([128, 8, L])
        Bd = Buf("dtmp")
        k.op("dve", lambda: V.tensor_copy(out=dtmp, in_=hn_fm), reads=[B_hn], writes=[Bd])
        dump(g, b, "d_hn", dtmp.rearrange("p k t -> p (k t)"), [Bd])

    for Tt in range(4):
        for nt in range(4):
            pi = 2 + (Tt * 4 + nt) % 2

            def mm():
                ins = None
                for kt in range(8):
                    ins = T_.matmul(pst[pi][:, :], lhsT=Wu[:, kt, Tt * 128:(Tt + 1) * 128],
                                    rhs=hn_fm[:, kt, nt * 512:(nt + 1) * 512], start=(kt == 0), stop=(kt == 7))
                return ins
            k.op("pe", mm, reads=[B_Wu, B_hn], writes=[PB[pi]])
            k.op("act", lambda: A_.copy(
                out=OGU[:, Tt, :].rearrange("p (s j) -> p s j", s=8)[:, :, nt * 64:(nt + 1) * 64],
                in_=pst[pi][:, :].rearrange("p (j s) -> p s j", s=8)), reads=[PB[pi]], writes=[B_ogu])

    g.arena_reset(keep=4 * L * 2 + 32 * 7 * 128 * 2)
    U = carve([128, 32, 256], BF16)
    B_U = Buf("U")
    SH = carve([128, 2, 32, 256], BF16)
    B_SH = [Buf("SH0"), Buf("SH1")]
    if not hasattr(g, "B_usd"):
        g.B_usd, g.B_ysd = Buf("us_d"), Buf("ys_d")
    k.dma(g.us_d.rearrange("(t p) n -> p t n", p=128), OGU, reads=[B_ogu], writes=[g.B_usd])
    for gi in range(32):
        k.dma(U[:, gi, :], g.us_d[gi * 16:(gi + 1) * 16, :].rearrange("c (s j) -> s c j", s=8),
              reads=[g.B_usd], writes=[B_U])
    for gi in range(32):
        pi = 2 + gi % 2

        def mm():
            ins = None
            for d in range(2):
                ins = T_.matmul(pst[pi][:, d * 256:(d + 1) * 256], lhsT=s5w[:, gi, 1 + d, :], rhs=U[:, gi, :],
                                start=True, stop=True)
            return ins
        k.op("pe", mm, reads=[B_w, B_U], writes=[PB[pi]])
        k.op("act", lambda: A_.copy(out=SH[:, :, gi, :], in_=pst[pi][:, :].rearrange("p (d j) -> p d j", d=2)),
             reads=[PB[pi]], writes=B_SH)
    rec_off = g.st["off"]
    for d in range(2):
        E = "dve" if d == 0 else "pool"
        EN = V if d == 0 else nc.gpsimd
        sh3 = [128, 32, 16]
        Xa, Xb, Xs, T1, T2, Hin, HinS = (carve(sh3) for _ in range(7))
        Ya, Yb, Ys, U1, U2 = (carve([128, 32]) for _ in range(5))
        C1 = carve([128, 2, 16, 16])
        C2 = carve([128, 2, 16, 16])
        bX = {id(Xa): Buf("Xa%d" % d), id(Xb): Buf("Xb%d" % d)}
        bXs, bT1, bT2, bHin, bC = Buf("Xs"), Buf("T1"), Buf("T2"), Buf("Hin"), Buf("C12")
        bY = {id(Ya): Buf("Ya"), id(Yb): Buf("Yb")}
        bYs, bU1, bU2 = Buf("Ys"), Buf("U1"), Buf("U2")
        SHv = SH[:, d].rearrange("p g (J i) -> p g J i", i=16)
        Ar = g.lam8r[:, d, :].unsqueeze(2).to_broadcast(sh3)
        Ai = g.lam8i[:, d, :].unsqueeze(2).to_broadcast(sh3)
        k.op(E, lambda: EN.memset(Xa, 0.0), writes=[bX[id(Xa)]])
        X, Xn = Xa, Xb
        for t in range(16):
            i = t if d == 0 else 15 - t
            bx, bxn = bX[id(X)], bX[id(Xn)]
            k.op(E, lambda: EN.tensor_copy(out=Xs[0:64], in_=X[64:128]), reads=[bx], writes=[bXs])
            k.op(E, lambda: EN.tensor_copy(out=Xs[64:128], in_=X[0:64]), reads=[bx], writes=[bXs])
            k.op(E, lambda: EN.tensor_tensor(out=T1, in0=X, in1=Ar, op=ALU.mult), reads=[bx, BC], writes=[bT1])
            k.op(E, lambda: EN.tensor_tensor(out=T2, in0=Xs, in1=Ai, op=ALU.mult), reads=[bXs, BC], writes=[bT2])
            k.op(E, lambda: EN.tensor_tensor(out=T1, in0=T1, in1=T2, op=ALU.add), reads=[bT1, bT2], writes=[bT1])
            k.op(E, lambda: EN.tensor_tensor(out=Xn, in0=T1, in1=SHv[:, :, :, i], op=ALU.add),
                 reads=[bT1, B_SH[d]], writes=[bxn])
            k.op(E, lambda: EN.tensor_copy(out=SHv[:, :, :, i], in_=X), reads=[bx], writes=[B_SH[d]])
            X, Xn = Xn, X
        bE = bX[id(X)]
        A128r = g.tabr[:, d, :, 16]
        A128i = g.tabi[:, d, :, 16]
        k.op(E, lambda: EN.memset(Ya, 0.0), writes=[bY[id(Ya)]])
        Y, Yn = Ya, Yb
        for t in range(16):
            J = t if d == 0 else 15 - t
            by, byn = bY[id(Y)], bY[id(Yn)]
            k.op(E, lambda: EN.tensor_copy(out=Hin[:, :, J], in_=Y), reads=[by], writes=[bHin])
            k.op(E, lambda: EN.tensor_copy(out=Ys[0:64], in_=Y[64:128]), reads=[by], writes=[bYs])
            k.op(E, lambda: EN.tensor_copy(out=Ys[64:128], in_=Y[0:64]), reads=[by], writes=[bYs])
            k.op(E, lambda: EN.tensor_tensor(out=U1, in0=Y, in1=A128r, op=ALU.mult), reads=[by, BC], writes=[bU1])
            k.op(E, lambda: EN.tensor_tensor(out=U2, in0=Ys, in1=A128i, op=ALU.mult), reads=[bYs, BC], writes=[bU2])
            k.op(E, lambda: EN.tensor_tensor(out=U1, in0=U1, in1=U2, op=ALU.add), reads=[bU1, bU2], writes=[bU1])
            k.op(E, lambda: EN.tensor_tensor(out=Yn, in0=U1, in1=X[:, :, J], op=ALU.add), reads=[bU1, bE],
                 writes=[byn])
            Y, Yn = Yn, Y
        k.op(E, lambda: EN.tensor_copy(out=HinS[0:64], in_=Hin[64:128]), reads=[bHin], writes=[bHin])
        k.op(E, lambda: EN.tensor_copy(out=HinS[64:128], in_=Hin[0:64]), reads=[bHin], writes=[bHin])
        for gs in range(16):
            gsl = slice(2 * gs, 2 * gs + 2)
            sh4 = [128, 2, 16, 16]
            tr_b = g.tabr[:, d, gsl, 0:16].unsqueeze(2).to_broadcast(sh4)
            ti_b = g.tabi[:, d, gsl, 0:16].unsqueeze(2).to_broadcast(sh4)
            k.op(E, lambda: EN.tensor_tensor(out=C1, in0=Hin[:, gsl, :].unsqueeze(3).to_broadcast(sh4), in1=tr_b,
                                             op=ALU.mult), reads=[bHin, BC, bC], writes=[bC])
            k.op(E, lambda: EN.tensor_tensor(out=C2, in0=HinS[:, gsl, :].unsqueeze(3).to_broadcast(sh4), in1=ti_b,
                                             op=ALU.mult), reads=[bHin, BC, bC], writes=[bC])
            k.op(E, lambda: EN.tensor_tensor(out=C1, in0=C1, in1=C2, op=ALU.add), reads=[bC], writes=[bC])
            k.op(E, lambda: EN.tensor_tensor(out=SHv[:, gsl], in0=SHv[:, gsl], in1=C1, op=ALU.add),
                 reads=[bC, B_SH[d]], writes=[B_SH[d]])
    g.arena_reset(keep=rec_off)
    ysb = [carve([128, 256]) for _ in range(2)]
    y2 = [carve([128, 256]) for _ in range(2)]
    gbf = [carve([128, 256], BF16) for _ in range(2)]
    sg = [carve([128, 256]) for _ in range(2)]
    sqb = [carve([128, 256], BF16) for _ in range(2)]
    B_y = [Buf("ysb0"), Buf("ysb1")]
    B_y2 = [Buf("y20"), Buf("y21")]
    B_g = [Buf("gbf0"), Buf("gbf1")]
    B_sg = [Buf("sg0"), Buf("sg1")]
    B_sq = [Buf("sq0"), Buf("sq1")]
    OG = OGU.rearrange("p t n -> p (t n)").rearrange("p (g j) -> p g j", g=32)
    def so1(gi):
        i = gi % 2
        py = 4 + i
        pg = 2 + i

        def mmy():
            T_.matmul(pst[py][:, 0:256], lhsT=s5w[:, gi, 0, :], rhs=U[:, gi, :], start=True, stop=False)
            T_.matmul(pst[py][:, 0:256], lhsT=s5w[:, gi, 3, :], rhs=SH[:, 0, gi, :], start=False, stop=False)
            return T_.matmul(pst[py][:, 0:256], lhsT=s5w[:, gi, 4, :], rhs=SH[:, 1, gi, :], start=False, stop=True)
        k.op("pe", mmy, reads=[B_w, B_U] + B_SH, writes=[PB[py]])
        k.op("act", lambda: A_.copy(out=ysb[i], in_=pst[py][:, 0:256]), reads=[PB[py]], writes=[B_y[i]])
        if gi == 0:
            dump(g, b, "d_y5", ysb[i], [B_y[i]])
        k.op("dve", lambda: V.tensor_tensor(out=y2[i], in0=ysb[i], in1=ysb[i], op=ALU.mult), reads=[B_y[i]],
             writes=[B_y2[i]])
        k.op("dve", lambda: V.tensor_scalar(out=y2[i], in0=y2[i], scalar1=0.044715, scalar2=1.0, op0=ALU.mult,
                                            op1=ALU.add), reads=[B_y2[i]], writes=[B_y2[i]])
        k.op("dve", lambda: V.tensor_tensor(out=y2[i], in0=y2[i], in1=ysb[i], op=ALU.mult), reads=[B_y[i], B_y2[i]],
             writes=[B_y2[i]])
        k.op("act", lambda: A_.activation(out=y2[i], in_=y2[i], func=AF.Tanh, scale=0.7978845608028654),
             reads=[B_y2[i]], writes=[B_y2[i]])
        k.op("dve", lambda: V.scalar_tensor_tensor(out=gbf[i], in0=y2[i], scalar=1.0, in1=ysb[i], op0=ALU.add,
                                                   op1=ALU.mult), reads=[B_y2[i], B_y[i]], writes=[B_g[i]])


    def so2(gi):
        i = gi % 2
        py = 4 + i
        pg = 2 + i
        def mmg():
            T_.matmul(pst[pg][:, 0:256], lhsT=s5w[:, gi, 5, :], rhs=gbf[i], start=True, stop=True)
            return T_.matmul(pst[pg][:, 256:512], lhsT=s5w[:, gi, 6, :], rhs=gbf[i], start=True, stop=True)
        k.op("pe", mmg, reads=[B_w, B_g[i]], writes=[PB[pg]])
        k.op("act", lambda: A_.activation(out=sg[i], in_=pst[pg][:, 256:512], func=AF.Tanh, scale=0.5,
                                          bias=g.gbias[:, gi, 1:2]), reads=[PB[pg], BC], writes=[B_sg[i]])
        k.op("dve", lambda: V.tensor_scalar(out=sg[i], in0=sg[i], scalar1=0.5, scalar2=0.5, op0=ALU.mult,
                                            op1=ALU.add), reads=[B_sg[i]], writes=[B_sg[i]])
        k.op("dve", lambda: V.scalar_tensor_tensor(out=OG[:, gi, :], in0=pst[pg][:, 0:256],
                                                   scalar=g.gbias[:, gi, 0:1], in1=sg[i], op0=ALU.add, op1=ALU.mult),
             reads=[PB[pg], B_sg[i], BC], writes=[B_ogu])
        k.op("act", lambda: A_.activation(out=sqb[i], in_=OG[:, gi, :], func=AF.Square), reads=[B_ogu],
             writes=[B_sq[i]])
        k.op("pe", lambda: T_.matmul(pst[6][:, 0:256], lhsT=g.blk16_b[:], rhs=sqb[i], start=(gi == 0),
                                     stop=(gi == 31)), reads=[B_sq[i], BC], writes=[PB[6]])

    for gi in range(32):
        so1(gi)
        so2(gi)
    rs = carve([128, 256])
    B_rs = Buf("rs")
    k.op("act", lambda: A_.activation(out=rs, in_=pst[6][:, 0:256], func=AF.Ln, scale=1.0 / 512, bias=g.eps_t[:]),
         reads=[PB[6], BC], writes=[B_rs])
    k.op("act", lambda: A_.activation(out=rs, in_=rs, func=AF.Exp, scale=-0.5), reads=[B_rs], writes=[B_rs])
    rsb = carve([128, 256], BF16)
    k.op("dve", lambda: V.tensor_copy(out=rsb, in_=rs), reads=[B_rs], writes=[B_rs])
    k.op("dve", lambda: V.tensor_tensor(out=OG, in0=OG, in1=rsb.unsqueeze(1).to_broadcast([128, 32, 256]),
                                        op=ALU.mult), reads=[B_ogu, B_rs], writes=[B_ogu])
    Yp = U.rearrange("p g j -> p (g j)").rearrange("p (t q j) -> p t q j", t=4, q=8)
    for gi in range(32):
        k.dma(g.ys_d[gi * 16:(gi + 1) * 16, :].rearrange("d (q j) -> q d j", q=8), OG[:, gi, :],
              reads=[B_ogu], writes=[g.B_ysd])
    k.dma(U.rearrange("p g j -> p (g j)").rearrange("p (t n) -> p t n", t=4),
          g.ys_d.rearrange("(t p) n -> p t n", p=128), reads=[g.B_ysd], writes=[B_U])
    for Tt in range(4):
        k.op("dve", lambda: V.tensor_scalar(out=g.ys5_fm[:, Tt, :].rearrange("p (j q) -> p q j", q=8),
                                            in0=Yp[:, Tt, :, :], scalar1=g.ns5_fm[:, Tt:Tt + 1], scalar2=None,
                                            op0=ALU.mult), reads=[B_U, BC], writes=[g.B_ys5])
    if b == 0 and "d_ys5" in g.dbg_out:
        dtmp = carve([128, 4, L])
        Bd = Buf("dtmp")
        k.op("dve", lambda: V.tensor_copy(out=dtmp, in_=g.ys5_fm[:]), reads=[g.B_ys5], writes=[Bd])
        dump(g, b, "d_ys5", dtmp.rearrange("p k t -> p (k t)"), [Bd])


def _rmsnorm_stats(g, src, stat, junk, rd, B_stat, B_junk, scale):
    k, A_ = g.k, g.nc.scalar
    k.op("act", lambda: A_.activation(out=junk, in_=src, func=AF.Square, accum_out=stat[:, 0:1]),
         reads=rd, writes=[B_junk, B_stat])
    k.op("act", lambda: A_.activation(out=stat[:, 1:2], in_=stat[:, 0:1], func=AF.Ln, scale=scale, bias=g.eps_t[:]),
         reads=[B_stat, g.B_const], writes=[B_stat])
    k.op("act", lambda: A_.activation(out=stat[:, 2:3], in_=stat[:, 1:2], func=AF.Exp, scale=-0.5),
         reads=[B_stat], writes=[B_stat])


def phase_b(g, b):
    nc, k, P = g.nc, g.k, g.P
    V, A_, T_ = nc.vector, nc.scalar, nc.tensor
    carve, pst, PB, BC = g.carve, g.pst, g.PB, g.B_const
    if not hasattr(g, "B_dram"):
        g.B_dram = {n: Buf(n) for n in ("xs", "btm", "bfm", "cfm", "zs", "h1", "hn2", "g")}
    BD = g.B_dram
    g.arena_reset()
    dt_t = carve([128, NCH, 32])
    dta = carve([128, NCH, 32])
    rtmp = carve([128, NCH, 32])
    dsp = [carve([128, NCH, 32], BF16) for _ in range(3)]
    B_dt = Buf("dt")
    keep = g.st["off"]

    hn_fm = carve([128, 8, L], BF16)
    B_hn = Buf("hn_fm_b")
    for kt in range(8):
        k.dma(hn_fm[:, kt, :], g.hnfm_d[kt * 128:(kt + 1) * 128, :], reads=[g.B_hnfm_d], writes=[B_hn])
    Wz = carve([128, 8, D], BF16)
    Wdt = carve([128, 8, 32], BF16)
    B_Wz = Buf("Wz")
    k.dma(Wz, g.win_b[:, 0:D].rearrange("(kt p) n -> p kt n", p=128), reads=[g.wb_bufs["win"]], writes=[B_Wz])
    k.dma(Wdt, g.win_b[:, 3072:3104].rearrange("(kt p) n -> p kt n", p=128), reads=[g.wb_bufs["win"]], writes=[B_Wz])
    wt = [carve([128, 8, 128], BF16) for _ in range(2)]
    B_wt = [Buf("wt0"), Buf("wt1")]
    pre = [carve([128, L + 4]) for _ in range(2)]
    B_pre = [Buf("pre0"), Buf("pre1")]
    acc = [carve([128, L]) for _ in range(2)]
    B_acc = [Buf("acc0"), Buf("acc1")]
    xc = [carve([128, L], BF16) for _ in range(2)]
    B_xc = [Buf("xc0"), Buf("xc1")]
    xst = [carve([128, NCH, 128], BF16) for _ in range(2)]
    B_xst = [Buf("xst0"), Buf("xst1")]
    zst = [carve([128, D], BF16) for _ in range(2)]
    B_zst = [Buf("zst0"), Buf("zst1")]
    for i in range(2):
        k.op("dve", lambda: V.memset(pre[i][:, 0:2], 0.0), writes=[B_pre[i]])
        k.op("dve", lambda: V.memset(pre[i][:, L + 2:L + 4], 0.0), writes=[B_pre[i]])
    cw, cb = g.cw_ssd, g.cb_ssd

    def wload(m):
        k.dma(wt[m % 2], g.win_b[:, D + m * 128:D + (m + 1) * 128].rearrange("(kt p) n -> p kt n", p=128),
              reads=[g.wb_bufs["win"]], writes=[B_wt[m % 2]])

    def xs1(m):
        i = m % 2
        if m == 0:
            wload(0)
        if m + 1 < 16:
            wload(m + 1)
        for nt in range(4):
            pi = 2 + nt % 2

            def mm():
                ins = None
                for kt in range(8):
                    ins = T_.matmul(pst[pi][:, :], lhsT=wt[i][:, kt, :], rhs=hn_fm[:, kt, nt * 512:(nt + 1) * 512],
                                    start=(kt == 0), stop=(kt == 7))
                return ins
            k.op("pe", mm, reads=[B_wt[i], B_hn], writes=[PB[pi]])
            k.op("act", lambda: A_.copy(out=pre[i][:, 2 + nt * 512:2 + (nt + 1) * 512], in_=pst[pi][:, :]),
                 reads=[PB[pi]], writes=[B_pre[i]])

    def xs2(m):
        i = m % 2
        k.op("act", lambda: A_.activation(out=acc[i], in_=pre[i][:, 2:L + 2], func=AF.Identity, scale=cw[:, m, 2:3],
                                          bias=cb[:, m:m + 1]), reads=[B_pre[i], BC], writes=[B_acc[i]])
        for tap in (0, 1, 3, 4):
            k.op("dve", lambda: V.scalar_tensor_tensor(out=acc[i], in0=pre[i][:, tap:tap + L],
                                                       scalar=cw[:, m, tap:tap + 1], in1=acc[i], op0=ALU.mult,
                                                       op1=ALU.add), reads=[B_pre[i], B_acc[i], BC],
                 writes=[B_acc[i]])

    def xs3(m):
        i = m % 2
        k.op("act", lambda: A_.activation(out=xc[i], in_=acc[i], func=AF.Silu), reads=[B_acc[i]], writes=[B_xc[i]])
        if m >= 8:
            dst, bn = (g.bfm_d, "bfm") if m < 12 else (g.cfm_d, "cfm")
            r0 = (m - 8) % 4 * 128
            k.dma(dst[r0:r0 + 128, :], xc[i], reads=[B_xc[i]], writes=[BD[bn]])
        if m < 12:
            for half in range(2):
                pb = 4 + half
                psb = pst[pb][:, :].bitcast(BF16)

                def tr():
                    ins = None
                    for cc in range(8):
                        c = half * 8 + cc
                        ins = T_.transpose(out=psb[:, cc * 128:(cc + 1) * 128], in_=xc[i][:, c * 128:(c + 1) * 128],
                                           identity=g.ident_b[:])
                    return ins
                k.op("pe", tr, reads=[B_xc[i], BC], writes=[PB[pb]])
                k.op("dve", lambda: V.tensor_copy(out=xst[i][:, half * 8:(half + 1) * 8, :],
                                                  in_=psb.rearrange("p (c t) -> p c t", c=8)),
                     reads=[PB[pb]], writes=[B_xst[i]])
            if m < 8:
                k.dma(g.xs_d.rearrange("(c t) d -> t c d", t=128)[:, :, m * 128:(m + 1) * 128], xst[i],
                      reads=[B_xst[i]], writes=[BD["xs"]])
            else:
                k.dma(g.btm_d.rearrange("(c t) d -> t c d", t=128)[:, :, (m - 8) * 128:(m - 7) * 128], xst[i],
                      reads=[B_xst[i]], writes=[BD["btm"]])

    pipeline(16, [(0, xs1), (1, xs2), (2, xs3)], "b1")
    for c in range(NCH):
        i = c % 2

        def mmz():
            ins = None
            for half in range(2):
                for kt in range(8):
                    ins = T_.matmul(pst[6 + half][:, :], lhsT=hn_fm[:, kt, c * 128:(c + 1) * 128],
                                    rhs=Wz[:, kt, half * 512:(half + 1) * 512], start=(kt == 0), stop=(kt == 7))
            return ins
        k.op("pe", mmz, reads=[B_hn, B_Wz], writes=[PB[6], PB[7]])
        for half in range(2):
            k.op("act", lambda: A_.activation(out=zst[i][:, half * 512:(half + 1) * 512], in_=pst[6 + half][:, :],
                                              func=AF.Silu), reads=[PB[6 + half]], writes=[B_zst[i]])
        k.dma(g.zs_d[c * 128:(c + 1) * 128, :], zst[i], reads=[B_zst[i]], writes=[BD["zs"]])

        def mmd():
            ins = None
            for kt in range(8):
                ins = T_.matmul(pst[1][:, 0:32], lhsT=hn_fm[:, kt, c * 128:(c + 1) * 128], rhs=Wdt[:, kt, :],
                                start=(kt == 0), stop=(kt == 7))
            return ins
        k.op("pe", mmd, reads=[B_hn, B_Wz], writes=[PB[1]])
        k.op("dve", lambda: V.tensor_tensor(out=dt_t[:, c, :], in0=pst[1][:, 0:32], in1=g.dtb_bc[:], op=ALU.add),
             reads=[PB[1], BC], writes=[B_dt])
    RWd = dict(reads=[B_dt, BC], writes=[B_dt])
    k.op("act", lambda: A_.activation(out=dt_t, in_=dt_t, func=AF.Exp), **RWd)
    k.op("act", lambda: A_.activation(out=dt_t, in_=dt_t, func=AF.Ln, bias=g.one_t[:]), **RWd)
    k.op("dve", lambda: V.tensor_tensor(out=dta, in0=dt_t, in1=g.a_bc[:].unsqueeze(1).to_broadcast([128, NCH, 32]),
                                        op=ALU.mult), **RWd)
    k.op("dve", lambda: V.tensor_copy(out=dsp[0], in_=dta), **RWd)
    k.op("dve", lambda: V.tensor_tensor(out=rtmp, in0=dta, in1=dsp[0], op=ALU.subtract), **RWd)
    k.op("dve", lambda: V.tensor_copy(out=dsp[1], in_=rtmp), **RWd)
    k.op("dve", lambda: V.tensor_tensor(out=rtmp, in0=rtmp, in1=dsp[1], op=ALU.subtract), **RWd)
    k.op("dve", lambda: V.tensor_copy(out=dsp[2], in_=rtmp), **RWd)

    g.arena_reset(keep=keep)
    Wout = carve([128, 12, D], BF16)
    B_Wout = Buf("Wout")
    for j in range(3):
        k.dma(Wout[:, 4 * j:4 * j + 4, :], g.wout_b[512 * j:512 * (j + 1), :].rearrange("(kt p) n -> p kt n", p=128),
              reads=[g.wb_bufs["wout"]], writes=[B_Wout])
    Hbs = carve([128, NCH, D], BF16)
    B_Hbs = Buf("Hbs")
    Hst = [carve([128, D]) for _ in range(2)]
    B_Hst = [Buf("Hst0"), Buf("Hst1")]
    Hbf = carve([128, D], BF16)
    B_Hbf = Buf("Hbf")
    xs = [carve([128, D], BF16) for _ in range(2)]
    btm = [carve([128, 512], BF16) for _ in range(2)]
    bfm = [carve([128, 4, 128], BF16) for _ in range(2)]
    cfm = [carve([128, 4, 128], BF16) for _ in range(2)]
    zs = [carve([128, D], BF16) for _ in range(2)]
    xres = [carve([128, D]) for _ in range(2)]
    B_ld = [{n: Buf(n + str(i)) for n in ("xs", "btm", "bfm", "cfm", "zs", "xres")} for i in range(2)]
    Atm = carve([128, NCH, 32])
    tmpe = carve([128, NCH, 32])
    wq = carve([128, NCH, 32])
    dlast = carve([128, NCH, 32])
    B_sm = Buf("small")
    A2E = [carve([128, NCH, 128], BF16) for _ in range(2)]
    nA2 = [carve([128, NCH, 128], BF16) for _ in range(2)]
    B_A3 = [Buf("A30"), Buf("A31")]
    rt1 = carve([128, 512])
    Gm = [carve([128, 4, 128], BF16) for _ in range(2)]
    B_Gm = Buf("Gm")
    Lb = [carve([128, 4, 128], BF16) for _ in range(2)]
    B_Lb = [Buf("Lb0"), Buf("Lb1")]
    GL = [carve([128, 16, 128], BF16) for _ in range(2)]
    Cd = [carve([128, 16, 128], BF16) for _ in range(2)]
    B_GL = [Buf("GL0"), Buf("GL1")]
    B_Cd = [Buf("Cd0"), Buf("Cd1")]
    xdt = [carve([128, D], BF16) for _ in range(2)]
    xw = [carve([128, D], BF16) for _ in range(2)]
    B_xdt = [Buf("xdt0"), Buf("xdt1")]
    B_xw = [Buf("xw0"), Buf("xw1")]
    y1 = carve([128, D])
    h1 = y1
    ynb = carve([128, D], BF16)
    hn2b = ynb
    ysfm = carve([128, 8, 128], BF16)
    hn2f = carve([128, 8, 128], BF16)
    stat = carve([128, 4])
    stat2 = carve([128, 4])
    junk = carve([128, D], BF16)
    B_y1, B_ynb, B_ysfm, B_hn2f = (Buf(n) for n in ("y1", "ynb", "ysfm", "hn2f"))
    B_y2, B_h1, B_hn2b = B_y1, B_y1, B_ynb
    B_stat, B_stat2, B_junk = Buf("stat"), Buf("stat2"), Buf("junkb")

    h3 = lambda a: a.rearrange("p (h e) -> p h e", h=16)
    hb = lambda a: a.unsqueeze(2).to_broadcast([128, 16, 64])

    def mmA():
        ins = None
        for c in range(NCH):
            for kk in range(3):
                ins = T_.matmul(pst[1][:, c * 32:c * 32 + 16], lhsT=g.maskF_b[:], rhs=dsp[kk][:, c, 0:16],
                                start=(kk == 0), stop=(kk == 2))
            for kk in range(3):
                ins = T_.matmul(pst[1][:, c * 32 + 16:c * 32 + 32], lhsT=g.maskB_b[:], rhs=dsp[kk][:, c, 16:32],
                                start=(kk == 0), stop=(kk == 2))
            for kk in range(3):
                ins = T_.matmul(pst[0][:, c * 32:(c + 1) * 32], lhsT=g.ones_b[:], rhs=dsp[kk][:, c, :],
                                start=(kk == 0), stop=(kk == 2))
        return ins
    k.op("pe", mmA, reads=[B_dt, BC], writes=[PB[0], PB[1]])
    fl = lambda a: a.rearrange("p c h -> p (c h)")
    rws = dict(reads=[PB[0], PB[1], B_sm, B_dt], writes=[B_sm])
    k.op("dve", lambda: V.tensor_copy(out=fl(Atm), in_=pst[1][:, :]), **rws)
    k.op("dve", lambda: V.tensor_tensor(out=fl(tmpe), in0=pst[0][:, :], in1=fl(Atm), op=ALU.subtract), **rws)
    k.op("act", lambda: A_.activation(out=fl(tmpe), in_=fl(tmpe), func=AF.Exp), **rws)
    k.op("dve", lambda: V.tensor_tensor(out=fl(wq), in0=fl(tmpe), in1=fl(dt_t), op=ALU.mult), **rws)
    k.op("act", lambda: A_.activation(out=fl(dlast), in_=pst[0][:, :], func=AF.Exp), **rws)
    for d, mk in enumerate((g.maskF_b, g.maskB_b)):
        def mmF():
            ins = None
            for c in range(NCH):
                for kk in range(3):
                    ins = T_.matmul(pst[2 + c // 4][0:32, (c % 4) * 128:(c % 4 + 1) * 128], lhsT=dsp[kk][:, c, :],
                                    rhs=mk[:], start=(kk == 0), stop=(kk == 2))
            return ins
        k.op("pe", mmF, reads=[B_dt, BC], writes=[PB[2], PB[3], PB[4], PB[5]])
        for j in range(4):
            src = pst[2 + j][0:32, :]
            cs = slice(4 * j, 4 * j + 4)
            v3 = lambda a: a.rearrange("p c q -> p (c q)")
            rw3 = dict(reads=[PB[2 + j], B_A3[d]], writes=[B_A3[d]])
            k.op("dve", lambda: V.tensor_copy(out=v3(A2E[d][0:32, cs, :]), in_=src), **rw3)
            k.op("act", lambda: A_.activation(out=v3(A2E[d][64:96, cs, :]), in_=src, func=AF.Exp), **rw3)
            k.op("dve", lambda: V.tensor_tensor(out=rt1[0:32, :], in0=src, in1=v3(A2E[d][0:32, cs, :]),
                                                op=ALU.subtract), **rw3)
            k.op("dve", lambda: V.tensor_copy(out=v3(A2E[d][32:64, cs, :]), in_=rt1[0:32, :]), **rw3)
        rw3 = dict(reads=[B_A3[d]], writes=[B_A3[d]])
        k.op("dve", lambda: V.tensor_scalar(out=nA2[d][0:64].rearrange("p c q -> p (c q)"),
                                            in0=A2E[d][0:64].rearrange("p c q -> p (c q)"), scalar1=-1.0,
                                            scalar2=None, op0=ALU.mult), **rw3)

    def load_chunk(c, i, full):
        tmc = slice(c * 128, (c + 1) * 128)
        k.dma(xs[i], g.xs_d[tmc, :], reads=[BD["xs"]], writes=[B_ld[i]["xs"]])
        k.dma(btm[i], g.btm_d[tmc, :], reads=[BD["btm"]], writes=[B_ld[i]["btm"]])
        if full:
            k.dma(bfm[i], g.bfm_d.rearrange("(gq n) t -> n gq t", n=128)[:, :, tmc], reads=[BD["bfm"]],
                  writes=[B_ld[i]["bfm"]])
            k.dma(cfm[i], g.cfm_d.rearrange("(gq n) t -> n gq t", n=128)[:, :, tmc], reads=[BD["cfm"]],
                  writes=[B_ld[i]["cfm"]])
            k.dma(zs[i], g.zs_d[tmc, :], reads=[BD["zs"]], writes=[B_ld[i]["zs"]])
            k.dma(xres[i], g.x[b, tmc, :], writes=[B_ld[i]["xres"]])

    def state_update(d, i, c):
        def mm():
            ins = None
            for gq in range(4):
                ins = T_.matmul(pst[6 + gq // 2][:, (gq % 2) * 256:(gq % 2 + 1) * 256],
                                lhsT=btm[i][:, gq * 128:(gq + 1) * 128], rhs=xw[d][:, gq * 256:(gq + 1) * 256],
                                start=True, stop=True)
            return ins
        k.op("pe", mm, reads=[B_ld[i]["btm"], B_xw[d]], writes=[PB[6], PB[7]])
        k.op("dve", lambda: V.tensor_tensor(out=h3(Hst[d]), in0=h3(Hst[d]), in1=hb(dlast[:, c, 16 * d:16 * d + 16]),
                                            op=ALU.mult), reads=[B_Hst[d], B_sm], writes=[B_Hst[d]])
        for half in range(2):
            sl = slice(half * 512, (half + 1) * 512)
            k.op("dve", lambda: V.tensor_tensor(out=Hst[d][:, sl], in0=Hst[d][:, sl], in1=pst[6 + half][:, :],
                                                op=ALU.add), reads=[B_Hst[d], PB[6 + half]], writes=[B_Hst[d]])

    for d in range(2):
        k.op("dve", lambda: V.memset(Hst[d], 0.0), writes=[B_Hst[d]])
    for n_, c in enumerate(range(NCH - 1, -1, -1)):
        i = n_ % 2
        load_chunk(c, i, False)
        k.op("dve", lambda: V.tensor_copy(out=Hbs[:, c, :], in_=Hst[1]), reads=[B_Hst[1]], writes=[B_Hbs])
        k.op("dve", lambda: V.tensor_tensor(out=h3(xw[1]), in0=h3(xs[i]), in1=hb(wq[:, c, 16:32]), op=ALU.mult),
             reads=[B_ld[i]["xs"], B_sm], writes=[B_xw[1]])
        state_update(1, i, c)
    def core(c):
        i = c % 2
        load_chunk(c, i, True)
        yield

        def mmg():
            ins = None
            for gq in range(4):
                ins = T_.matmul(pst[0][:, gq * 128:(gq + 1) * 128], lhsT=bfm[i][:, gq, :], rhs=cfm[i][:, gq, :],
                                start=True, stop=True)
            return ins
        k.op("pe", mmg, reads=[B_ld[i]["bfm"], B_ld[i]["cfm"]], writes=[PB[0]])
        yield
        for d, mk in enumerate((g.maskF_f, g.maskB_f)):
            k.op("dve", lambda: V.tensor_tensor(out=Gm[d], in0=pst[0][:, :].rearrange("p (a q) -> p a q", a=4),
                                                in1=mk[:].unsqueeze(1).to_broadcast([128, 4, 128]), op=ALU.mult),
                 reads=[PB[0], BC], writes=[B_Gm])
            yield
            k.op("dve", lambda: V.tensor_tensor(out=h3(xdt[d]), in0=h3(xs[i]),
                                                in1=hb(dt_t[:, c, 16 * d:16 * d + 16]), op=ALU.mult),
                 reads=[B_ld[i]["xs"], B_dt], writes=[B_xdt[d]])
            yield
        k.op("dve", lambda: V.tensor_tensor(out=h3(xw[0]), in0=h3(xs[i]), in1=hb(wq[:, c, 0:16]), op=ALU.mult),
             reads=[B_ld[i]["xs"], B_sm], writes=[B_xw[0]])
        yield
        k.op("dve", lambda: V.tensor_copy(out=Hbf, in_=Hst[0]), reads=[B_Hst[0]], writes=[B_Hbf])
        yield
        nb_ = 0
        for d in range(2):
            for grp in range(4):
                pl, pd = (2, 6) if nb_ % 2 == 0 else (3, 7)
                lbi = nb_ % 2
                nb_ += 1

                def mmb():
                    ins = None
                    for i4 in range(4):
                        hh = d * 16 + grp * 4 + i4
                        es_ = g.esel_b[0:64, hh * 128:(hh + 1) * 128]
                        o = pst[pl][:, i4 * 128:(i4 + 1) * 128]
                        T_.matmul(o, lhsT=es_, rhs=A2E[d][0:64, c, :], start=True, stop=False)
                        T_.matmul(o, lhsT=nA2[d][0:64, c, :], rhs=es_, start=False, stop=True)
                        ins = T_.matmul(pst[pd][:, i4 * 128:(i4 + 1) * 128],
                                        lhsT=g.esel_b[64:96, hh * 128:(hh + 1) * 128], rhs=A2E[d][64:96, c, :],
                                        start=True, stop=True)
                    return ins
                k.op("pe", mmb, reads=[B_A3[d], BC], writes=[PB[pl], PB[pd]])
                yield
                k.op("act", lambda: A_.activation(out=Lb[lbi], in_=pst[pl][:, :].rearrange("p (a q) -> p a q", a=4),
                                                  func=AF.Exp), reads=[PB[pl]], writes=[B_Lb[lbi]])
                yield
                h0 = grp * 4
                k.op("dve", lambda: V.scalar_tensor_tensor(
                    out=GL[d][:, h0:h0 + 4, :], in0=Lb[lbi], scalar=1.0,
                    in1=Gm[d][:, grp, :].unsqueeze(1).to_broadcast([128, 4, 128]), op0=ALU.min, op1=ALU.mult),
                    reads=[B_Lb[lbi], B_Gm], writes=[B_GL[d]])
                yield
                k.op("dve", lambda: V.tensor_tensor(
                    out=Cd[d][:, h0:h0 + 4, :], in0=pst[pd][:, :].rearrange("p (a q) -> p a q", a=4),
                    in1=cfm[i][:, grp, :].unsqueeze(1).to_broadcast([128, 4, 128]), op=ALU.mult),
                    reads=[PB[pd], B_ld[i]["cfm"]], writes=[B_Cd[d]])
                yield

        def mmy():
            ins = None
            for h in range(16):
                o = pst[4 + h // 8][:, (h % 8) * 64:(h % 8 + 1) * 64]
                hs = slice(h * 64, (h + 1) * 64)
                T_.matmul(o, lhsT=GL[0][:, h, :], rhs=xdt[0][:, hs], start=True, stop=False)
                T_.matmul(o, lhsT=GL[1][:, h, :], rhs=xdt[1][:, hs], start=False, stop=False)
                T_.matmul(o, lhsT=Cd[0][:, h, :], rhs=Hbf[:, hs], start=False, stop=False)
                ins = T_.matmul(o, lhsT=Cd[1][:, h, :], rhs=Hbs[:, c, hs], start=False, stop=True)
            return ins
        k.op("pe", mmy, reads=B_GL + B_Cd + B_xdt + [B_Hbf, B_Hbs], writes=[PB[4], PB[5]])
        yield
        state_update(0, i, c)
        yield

    def epi(c):
        i = c % 2
        k.op("dve", lambda: V.tensor_tensor(out=h3(y1), in0=h3(xs[i]), in1=hb(g.dsk_bc[:]), op=ALU.mult),
             reads=[B_ld[i]["xs"], BC], writes=[B_y1])
        yield
        for half in range(2):
            sl = slice(half * 512, (half + 1) * 512)
            k.op("dve", lambda: V.tensor_tensor(out=y1[:, sl], in0=y1[:, sl], in1=pst[4 + half][:, :], op=ALU.add),
                 reads=[B_y1, PB[4 + half]], writes=[B_y1])
            yield
        k.op("dve", lambda: V.tensor_tensor(out=y1, in0=y1, in1=zs[i], op=ALU.mult), reads=[B_y1, B_ld[i]["zs"]],
             writes=[B_y2])
        yield
        _rmsnorm_stats(g, y1, stat, junk, [B_y2], B_stat, B_junk, 1.0 / D)
        yield
        k.op("dve", lambda: V.tensor_scalar(out=ynb, in0=y1, scalar1=stat[:, 2:3], scalar2=None, op0=ALU.mult),
             reads=[B_y2, B_stat], writes=[B_ynb])
        yield
        psb = pst[1][:, :].bitcast(BF16)

        def tr():
            ins = None
            for kt in range(8):
                ins = T_.transpose(out=psb[:, kt * 128:(kt + 1) * 128], in_=ynb[:, kt * 128:(kt + 1) * 128],
                                   identity=g.ident_b[:])
            return ins
        k.op("pe", tr, reads=[B_ynb, BC], writes=[PB[1]])
        yield
        k.op("dve", lambda: V.tensor_tensor(out=ysfm, in0=psb.rearrange("p (k t) -> p k t", k=8),
                                            in1=g.nssd_fm[:].unsqueeze(2).to_broadcast([128, 8, 128]), op=ALU.mult),
             reads=[PB[1], BC], writes=[B_ysfm])
        yield

        for half in range(2):
            ns = slice(half * 512, (half + 1) * 512)

            def mmo():
                ins = None
                for kt in range(8):
                    T_.matmul(pst[1][:, :], lhsT=ysfm[:, kt, :], rhs=Wout[:, kt, ns], start=(kt == 0), stop=False)
                for kt in range(4):
                    ins = T_.matmul(pst[1][:, :], lhsT=g.ys5_fm[:, kt, c * 128:(c + 1) * 128],
                                    rhs=Wout[:, 8 + kt, ns], start=False, stop=(kt == 3))
                return ins
            k.op("pe", mmo, reads=[B_ysfm, B_Wout, g.B_ys5], writes=[PB[1]])
            yield
            k.op("dve", lambda: V.tensor_tensor(out=h1[:, ns], in0=pst[1][:, :], in1=xres[i][:, ns], op=ALU.add),
                 reads=[PB[1], B_ld[i]["xres"]], writes=[B_h1])
            yield
        k.dma(g.h1_d[c * 128:(c + 1) * 128, :], h1, reads=[B_h1], writes=[BD["h1"]])
        yield
        if b == 0 and "d_h1" in g.dbg_out:
            k.dma(g.dbg_out["d_h1"][c * 128:(c + 1) * 128, :], h1, reads=[B_h1], writes=[Buf("dbg")])
            yield
        _rmsnorm_stats(g, h1, stat2, junk, [B_h1], B_stat2, B_junk, 1.0 / D)
        yield
        k.op("dve", lambda: V.tensor_scalar(out=hn2b, in0=h1, scalar1=stat2[:, 2:3], scalar2=None, op0=ALU.mult),
             reads=[B_h1, B_stat2], writes=[B_hn2b])
        yield
        psb3 = pst[1][:, :].bitcast(BF16)

        def tr2():
            ins = None
            for kt in range(8):
                ins = T_.transpose(out=psb3[:, kt * 128:(kt + 1) * 128], in_=hn2b[:, kt * 128:(kt + 1) * 128],
                                   identity=g.ident_b[:])
            return ins
        k.op("pe", tr2, reads=[B_hn2b, BC], writes=[PB[1]])
        yield
        k.op("dve", lambda: V.tensor_tensor(out=hn2f, in0=psb3.rearrange("p (k t) -> p k t", k=8),
                                            in1=g.nffn_fm[:].unsqueeze(2).to_broadcast([128, 8, 128]), op=ALU.mult),
             reads=[PB[1], BC], writes=[B_hn2f])
        yield
        k.dma(g.hn2fm_d.rearrange("(kt p) t -> p kt t", p=128)[:, :, c * 128:(c + 1) * 128], hn2f, reads=[B_hn2f],
              writes=[BD["hn2"]])
        yield

    def zipgen(gens):
        gens = [x for x in gens if x is not None]
        while gens:
            for x in list(gens):
                try:
                    next(x)
                except StopIteration:
                    gens.remove(x)

    import os
    for t in range(NCH + 1):
        ge = epi(t - 1) if t >= 1 else None
        gc = core(t) if t < NCH else None
        if os.environ.get("NOPIPE") or (os.environ.get("PIPE") is not None and "ssd" not in os.environ["PIPE"].split(",")):
            zipgen([ge])
            zipgen([gc])
        else:
            if ge is not None:
                for _ in range(4):
                    next(ge)
            zipgen([ge, gc])


def phase_d(g, b):
    nc, k, P = g.nc, g.k, g.P
    V, A_, T_ = nc.vector, nc.scalar, nc.tensor
    carve, pst, PB, BC = g.carve, g.pst, g.PB, g.B_const
    BD = g.B_dram
    g.arena_reset()
    hn2 = carve([128, 8, L], BF16)
    B_hn2 = Buf("hn2sb")
    for kt in range(8):
        k.dma(hn2[:, kt, :], g.hn2fm_d[kt * 128:(kt + 1) * 128, :], reads=[BD["hn2"]], writes=[B_hn2])
    wv = [carve([128, 8, 128], BF16) for _ in range(2)]
    wg = [carve([128, 8, 128], BF16) for _ in range(2)]
    B_w = [Buf("wvg0"), Buf("wvg1")]
    prev = [carve([128, L + 2]) for _ in range(2)]
    preg = [carve([128, L + 2]) for _ in range(2)]
    B_prev = [Buf("prev0"), Buf("prev1")]
    B_preg = [Buf("preg0"), Buf("preg1")]
    accv = [carve([128, L]) for _ in range(2)]
    accg = [carve([128, L]) for _ in range(2)]
    B_accv, B_accg = [Buf("accv0"), Buf("accv1")], [Buf("accg0"), Buf("accg1")]
    gt = [carve([128, L], BF16) for _ in range(2)]
    B_gt = [Buf("gt0"), Buf("gt1")]
    for i in range(2):
        for t_, bb in ((prev[i], B_prev[i]), (preg[i], B_preg[i])):
            k.op("dve", lambda: V.memset(t_[:, 0:1], 0.0), writes=[bb])
            k.op("dve", lambda: V.memset(t_[:, L + 1:L + 2], 0.0), writes=[bb])
    cw, cb = g.cw_ffn, g.cb_ffn
    NF = DFF // 128

    def fwload(m):
        i = m % 2
        k.dma(wv[i], g.wup_b[:, m * 128:(m + 1) * 128].rearrange("(kt p) n -> p kt n", p=128),
              reads=[g.wb_bufs["wup"]], writes=[B_w[i]])
        k.dma(wg[i], g.wup_b[:, DFF + m * 128:DFF + (m + 1) * 128].rearrange("(kt p) n -> p kt n", p=128),
              reads=[g.wb_bufs["wup"]], writes=[B_w[i]])

    def fs1(m):
        i = m % 2
        if m == 0:
            fwload(0)
        if m + 1 < NF:
            fwload(m + 1)
        for nt in range(4):
            for (wtile, dstt, bdst, pi) in ((wv[i], prev[i], B_prev[i], 2 + nt % 2),
                                            (wg[i], preg[i], B_preg[i], 4 + nt % 2)):
                def mm():
                    ins = None
                    for kt in range(8):
                        ins = T_.matmul(pst[pi][:, :], lhsT=wtile[:, kt, :], rhs=hn2[:, kt, nt * 512:(nt + 1) * 512],
                                        start=(kt == 0), stop=(kt == 7))
                    return ins
                k.op("pe", mm, reads=[B_w[i], B_hn2], writes=[PB[pi]])
                k.op("act", lambda: A_.copy(out=dstt[:, 1 + nt * 512:1 + (nt + 1) * 512], in_=pst[pi][:, :]),
                     reads=[PB[pi]], writes=[bdst])

    def fs2(m):
        i = m % 2
        for (src, bsrc, a_, ba, ci) in ((prev[i], B_prev[i], accv[i], B_accv[i], m),
                                        (preg[i], B_preg[i], accg[i], B_accg[i], NF + m)):
            k.op("act", lambda: A_.activation(out=a_, in_=src[:, 1:L + 1], func=AF.Identity, scale=cw[:, ci, 1:2],
                                              bias=cb[:, ci:ci + 1]), reads=[bsrc, BC], writes=[ba])
            for tap in (0, 2):
                k.op("dve", lambda: V.scalar_tensor_tensor(out=a_, in0=src[:, tap:tap + L],
                                                           scalar=cw[:, ci, tap:tap + 1], in1=a_, op0=ALU.mult,
                                                           op1=ALU.add), reads=[bsrc, ba, BC], writes=[ba])

    def fs3(m):
        i = m % 2
        k.op("act", lambda: A_.activation(out=accg[i], in_=accg[i], func=AF.Silu), reads=[B_accg[i]],
             writes=[B_accg[i]])
        k.op("dve", lambda: V.tensor_tensor(out=gt[i], in0=accg[i], in1=accv[i], op=ALU.mult),
             reads=[B_accg[i], B_accv[i]], writes=[B_gt[i]])
        k.dma(g.g_d[m * 128:(m + 1) * 128, :], gt[i], reads=[B_gt[i]], writes=[BD["g"]])

    pipeline(NF, [(0, fs1), (1, fs2), (1, fs3)], "ffn")
    g.arena_reset()
    Wdn = carve([128, NF, D], BF16)
    B_Wdn = Buf("Wdn")
    for j in range(0, NF, 2):
        k.dma(Wdn[:, j:j + 2, :], g.wdn_b[j * 128:(j + 2) * 128, :].rearrange("(kt p) n -> p kt n", p=128),
              reads=[g.wb_bufs["wdn"]], writes=[B_Wdn])
    gmt = [carve([128, NF, 512], BF16) for _ in range(2)]
    B_gmt = [Buf("gmt0"), Buf("gmt1")]
    h1t = [carve([128, D]) for _ in range(2)]
    B_h1t = [Buf("h1t0"), Buf("h1t1")]
    o1 = [carve([128, D]) for _ in range(2)]
    o2 = [carve([128, D]) for _ in range(2)]
    B_o1 = [Buf("o10"), Buf("o11")]
    B_o2 = [Buf("o20"), Buf("o21")]
    stat = [carve([128, 4]) for _ in range(2)]
    B_stat = [Buf("fst0"), Buf("fst1")]
    junk = carve([128, D], BF16)
    B_junk = Buf("fjunk")
    for nt in range(4):
        gi_ = nt % 2
        k.dma(gmt[gi_], g.g_d.rearrange("(kt p) t -> p kt t", p=128)[:, :, nt * 512:(nt + 1) * 512], reads=[BD["g"]],
              writes=[B_gmt[gi_]])
        for cc in range(4):
            c = nt * 4 + cc
            i = c % 2
            k.dma(h1t[i], g.h1_d[c * 128:(c + 1) * 128, :], reads=[BD["h1"]], writes=[B_h1t[i]])
            pb0 = 6 if i == 0 else 0

            def mm():
                ins = None
                for half in range(2):
                    for kt in range(NF):
                        ins = T_.matmul(pst[pb0 + half][:, :], lhsT=gmt[gi_][:, kt, cc * 128:(cc + 1) * 128],
                                        rhs=Wdn[:, kt, half * 512:(half + 1) * 512], start=(kt == 0),
                                        stop=(kt == NF - 1))
                return ins
            k.op("pe", mm, reads=[B_gmt[gi_], B_Wdn], writes=[PB[pb0], PB[pb0 + 1]])
            for half in range(2):
                sl = slice(half * 512, (half + 1) * 512)
                k.op("dve", lambda: V.tensor_tensor(out=o1[i][:, sl], in0=pst[pb0 + half][:, :], in1=h1t[i][:, sl],
                                                    op=ALU.add), reads=[PB[pb0 + half], B_h1t[i]], writes=[B_o1[i]])
            _rmsnorm_stats(g, o1[i], stat[i], junk, [B_o1[i]], B_stat[i], B_junk, 1.0 / D)
            k.op("dve", lambda: V.scalar_tensor_tensor(out=o2[i], in0=o1[i], scalar=stat[i][:, 2:3], in1=g.nfin_bc[:],
                                                       op0=ALU.mult, op1=ALU.mult),
                 reads=[B_o1[i], B_stat[i], BC], writes=[B_o2[i]])
            k.dma(g.out[b, c * 128:(c + 1) * 128, :], o2[i], reads=[B_o2[i]], writes=[Buf("outd")])


def prep_inputs(inputs, S):
    per_core = []
    hc = host_consts()
    xs = np.ascontiguousarray(inputs["x"], dtype=np.float32)
    ncore = xs.shape[0] // S
    for ci in range(ncore):
        m = {"x": xs[ci * S:(ci + 1) * S]}
        for n, v in inputs.items():
            if n == "x":
                continue
            a = np.ascontiguousarray(v, dtype=np.float32)
            if n != "norm_final_w":
                a = a[0]
            m[n] = np.ascontiguousarray(a)
        m.update(hc)
        per_core.append(m)
    return per_core


def kernel(**inputs):
    S = inputs["x"].shape[0] // NCORES
    nc = build_program(S)
    in_maps = prep_inputs(inputs, S)
    res = run_bass_kernel_spmd(nc, in_maps, core_ids=list(range(NCORES)))
    return np.concatenate([r["out"] for r in res.results], axis=0).astype(np.float32)
```

```python
import math
import numpy as np
import ml_dtypes
from contextlib import ExitStack
import concourse.bass as bass
import concourse.mybir as mybir
from concourse.bass_utils import run_bass_kernel_spmd

F32 = mybir.dt.float32
BF16 = mybir.dt.bfloat16
AF = mybir.ActivationFunctionType
ALU = mybir.AluOpType

NCORES = 8
L = 2048
D = 1024
NCH = 16
DIN = 3616
DFF = 2816
EPS = 1e-6
MAGIC = 12582912.0
TWO_PI = 2.0 * math.pi
EPOCH = 3000


class Buf:
    __slots__ = ("name", "writer", "readers")

    def __init__(self, name):
        self.name = name
        self.writer = None
        self.readers = []


class K:
    def __init__(self, nc, es):
        self.nc = nc
        self.es = es
        self.eng = {"pe": nc.tensor, "act": nc.scalar, "dve": nc.vector, "pool": nc.gpsimd, "sp": nc.sync}
        self.sems = {e: [] for e in self.eng}
        self.count = {e: 0 for e in self.eng}
        self.seen = {e: {} for e in self.eng}
        self.ndma = 88
        self.dsem = [es.enter_context(nc.semaphore("dsem%d" % i)) for i in range(self.ndma)]
        self.dval = [0] * self.ndma
        self.dnext = 0
        self.all_tokens = []

    def _sem(self, e, epoch):
        while len(self.sems[e]) <= epoch:
            self.sems[e].append(self.es.enter_context(self.nc.semaphore("s_%s_%d" % (e, len(self.sems[e])))))
        return self.sems[e][epoch]

    def _wait(self, e, tok):
        kind = tok[0]
        if kind == "eng":
            _, src, idx = tok
            if src == e and e == "pe":
                return
            if self.seen[e].get(("eng", src), -1) >= idx:
                return
            self.seen[e][("eng", src)] = idx
            self.eng[e].wait_ge(self._sem(src, idx // EPOCH), (idx % EPOCH) + 1)
        else:
            _, si, val = tok
            if self.seen[e].get(("dma", si), -1) >= val:
                return
            self.seen[e][("dma", si)] = val
            self.eng[e].wait_ge(self.dsem[si], val)

    def _deps(self, e, reads, writes):
        toks = []
        for b in reads:
            if b.writer is not None:
                toks.append(b.writer)
        for b in writes:
            if b.writer is not None:
                toks.append(b.writer)
            for r in b.readers:
                if not (r[0] == "eng" and r[1] == e):
                    toks.append(r)
        for t in toks:
            self._wait(e, t)

    def _commit(self, tok, reads, writes):
        for b in writes:
            b.writer = tok
            b.readers = []
        for b in reads:
            b.readers.append(tok)
            if len(b.readers) > 24:
                b.readers = b.readers[-24:]

    def op(self, e, fn, reads=(), writes=()):
        self._deps(e, reads, writes)
        ins = fn()
        idx = self.count[e]
        self.count[e] += 1
        ins.then_inc(self._sem(e, idx // EPOCH), 1)
        tok = ("eng", e, idx)
        self._commit(tok, reads, writes)
        return tok

    def dma(self, out, in_, reads=(), writes=(), q="sp", **kw):
        self._deps(q, reads, writes)
        si = self.dnext
        self.dnext = (self.dnext + 1) % self.ndma
        if self.dval[si] > 0:
            self._wait(q, ("dma", si, self.dval[si]))
        self.dval[si] += 16
        self.eng[q].dma_start(out=out, in_=in_, **kw).then_inc(self.dsem[si], 16)
        tok = ("dma", si, self.dval[si])
        self._commit(tok, reads, writes)
        self.all_tokens.append(tok)
        if len(self.all_tokens) > 64:
            self.all_tokens = self.all_tokens[-64:]
        return tok

    def barrier(self):
        toks = [("eng", s, self.count[s] - 1) for s in self.eng if self.count[s] > 0]
        toks += [("dma", i, self.dval[i]) for i in range(self.ndma) if self.dval[i] > 0]
        for e in self.eng:
            for t in toks:
                if t[0] == "eng" and t[1] == e:
                    continue
                self._wait(e, t)

    def finish(self):
        for t in [("dma", i, self.dval[i]) for i in range(self.ndma) if self.dval[i] > 0]:
            self._wait("sp", t)
        for s in ("pe", "act", "dve", "pool"):
            if self.count[s] > 0:
                self._wait("sp", ("eng", s, self.count[s] - 1))


def bc(ap, shape):
    return ap.to_broadcast(list(shape))


def host_consts():
    c = {}
    c["c_ident"] = np.eye(128, dtype=np.float32)
    s = np.arange(128)[:, None]
    q = np.arange(128)[None, :]
    c["c_maskF"] = (s <= q).astype(np.float32)
    c["c_maskB"] = (s >= q).astype(np.float32)
    c["c_mle8"] = ((s // 16) <= (q // 16)).astype(np.float32)
    c["c_mge8"] = ((s // 16) >= (q // 16)).astype(np.float32)
    c["c_blk16"] = ((s // 16) == (q // 16)).astype(np.float32)
    sel = np.zeros((128, 4), np.float32)
    sel[:64, 0] = 1.0
    sel[64:, 1] = 1.0
    sel[64:, 2] = -1.0
    sel[:64, 3] = -1.0
    sel[64:, 3] = 1.0
    c["c_sel"] = sel
    es = np.zeros((96, 32, 128), np.float32)
    for r in range(96):
        es[r, r % 32, :] = 1.0
    c["c_esel"] = es.reshape(96, 32 * 128)
    p8 = np.arange(8, dtype=np.float32)
    ev = np.stack([7 - p8, p8 - 7, p8 + 1, p8, -p8, 8 - p8], 0)
    c["c_evec"] = np.broadcast_to(ev.reshape(1, 48), (128, 48)).copy()
    i16 = np.arange(16, dtype=np.float32)
    ev2 = np.stack([np.concatenate([8 * i16, [128.0]]), np.concatenate([8 * (15 - i16), [128.0]])], 0)
    c["c_evec2"] = np.broadcast_to(ev2.reshape(1, 34), (128, 34)).copy().astype(np.float32)
    return c


def build_program(S, dbg=None, stop_after=None):
    nc = bass.Bass("TRN2", target_bir_lowering=False)
    es = ExitStack()

    def din(name, shape, dt=F32):
        return nc.dram_tensor(name, list(shape), dt, kind="ExternalInput").ap()

    x = din("x", [S, L, D])
    out = nc.dram_tensor("out", [S, L, D], F32, kind="ExternalOutput").ap()
    P = {}
    shapes = dict(
        norm_mix_w=[D], w_in=[D, DIN], ssd_conv_w=[5, 2048], ssd_conv_b=[2048],
        ssd_dt_bias_fwd=[16], ssd_dt_bias_bwd=[16], ssd_a_log_fwd=[16], ssd_a_log_bwd=[16],
        ssd_d=[16], ssd_norm_w=[D],
        s5_lambda_re_fwd=[32, 64], s5_lambda_im_fwd=[32, 64], s5_log_step_fwd=[32],
        s5_lambda_re_bwd=[32, 64], s5_lambda_im_bwd=[32, 64], s5_log_step_bwd=[32],
        s5_b_re=[32, 64, 16], s5_b_im=[32, 64, 16],
        s5_c_re_fwd=[32, 16, 64], s5_c_im_fwd=[32, 16, 64], s5_c_re_bwd=[32, 16, 64], s5_c_im_bwd=[32, 16, 64],
        s5_d=[512], s5_glu_w=[32, 16, 32], s5_glu_b=[32, 32], s5_norm_w=[512],
        w_out=[1536, D], norm_ffn_w=[D], ffn_w_up=[D, 2 * DFF], ffn_conv_w=[3, 2 * DFF], ffn_conv_b=[2 * DFF],
        ffn_w_down=[DFF, D], norm_final_w=[D],
    )
    for n, sh in shapes.items():
        P[n] = din(n, sh)
    C = {n: din(n, list(v.shape)) for n, v in host_consts().items()}
    dbg_out = {}
    if dbg:
        for n, sh in dbg.items():
            dbg_out[n] = nc.dram_tensor(n, list(sh), F32, kind="ExternalOutput").ap()

    def dscr(name, shape, dt=BF16):
        return nc.dram_tensor(name, list(shape), dt).ap()

    win_b = dscr("win_b", [D, DIN])
    wout_b = dscr("wout_b", [1536, D])
    wup_b = dscr("wup_b", [D, 2 * DFF])
    wdn_b = dscr("wdn_b", [DFF, D])
    hnfm_d = dscr("hnfm_d", [D, L])
    xs_d = dscr("xs_d", [L, D])
    btm_d = dscr("btm_d", [L, 512])
    bfm_d = dscr("bfm_d", [512, L])
    cfm_d = dscr("cfm_d", [512, L])
    zs_d = dscr("zs_d", [L, D])
    h1_d = dscr("h1_d", [L, D], F32)
    hn2fm_d = dscr("hn2fm_d", [D, L])
    g_d = dscr("g_d", [DFF, L])
    S5W = 32 * 7 * 128
    s5w_d = dscr("s5w_d", [128, S5W])
    us_d = dscr("us_d", [512, L])
    ys_d = dscr("ys_d", [512, L])

    with es:
        k = K(nc, es)
        sb = lambda name, shape, dt=F32: es.enter_context(nc.sbuf_tensor(name, list(shape), dt))
        ident_f = sb("ident_f", [128, 128])
        ident_b = sb("ident_b", [128, 128], BF16)
        maskF_b = sb("maskF_b", [128, 128], BF16)
        maskB_b = sb("maskB_b", [128, 128], BF16)
        maskF_f = sb("maskF_f", [128, 128])
        maskB_f = sb("maskB_f", [128, 128])
        ones_b = sb("ones_b", [128, 128], BF16)
        esel_b = sb("esel_b", [96, 32 * 128], BF16)
        nmix_fm = sb("nmix_fm", [128, 8])
        nssd_fm = sb("nssd_fm", [128, 8])
        nffn_fm = sb("nffn_fm", [128, 8])
        ns5_fm = sb("ns5_fm", [128, 4])
        nfin_bc = sb("nfin_bc", [128, D])
        cw_ssd = sb("cw_ssd", [128, 16, 5])
        cb_ssd = sb("cb_ssd", [128, 16])
        cw_ffn = sb("cw_ffn", [128, 44, 3])
        cb_ffn = sb("cb_ffn", [128, 44])
        a_bc = sb("a_bc", [128, 32])
        dtb_bc = sb("dtb_bc", [128, 32])
        dsk_bc = sb("dsk_bc", [128, 16])
        eps_t = sb("eps_t", [128, 1])
        one_t = sb("one_t", [128, 1])
        ys5_fm = sb("ys5_fm", [128, 4, L], BF16)
        ARENA = 164 * 1024
        arena = sb("arena", [128, ARENA // 2], BF16)
        pst = [es.enter_context(nc.psum_tensor("ps%d" % i, [128, 512], F32)) for i in range(8)]
        PB = [Buf("psb%d" % i) for i in range(8)]

        st = {"off": 0}

        def arena_reset(keep=0):
            k.barrier()
            st["off"] = keep

        def carve(shape, dt=F32):
            n = 1
            for d_ in shape[1:]:
                n *= d_
            nbytes = n * (4 if dt == F32 else 2)
            nbytes = (nbytes + 63) // 64 * 64
            o = st["off"]
            st["off"] += nbytes
            assert st["off"] <= ARENA, ("arena overflow", st["off"])
            v = arena[0:shape[0], o // 2:(o + nbytes) // 2]
            if dt == F32:
                v = v.bitcast(F32)[:, 0:n]
            else:
                v = v[:, 0:n]
            if len(shape) == 3:
                v = v.rearrange("p (a b) -> p a b", a=shape[1])
            elif len(shape) == 4:
                v = v.rearrange("p (a b c) -> p a b c", a=shape[1], b=shape[2])
            return v

        V, A_, G_, T_ = nc.vector, nc.scalar, nc.gpsimd, nc.tensor

        B_const = Buf("consts")
        wb_bufs = {n: Buf(n) for n in ("win", "wout", "wup", "wdn")}
        for (dst, src, rows, bn) in ((win_b, P["w_in"], D, "win"), (wout_b, P["w_out"], 1536, "wout"),
                                     (wup_b, P["ffn_w_up"], D, "wup"), (wdn_b, P["ffn_w_down"], DFF, "wdn")):
            step = 256
            for r0 in range(0, rows, step):
                k.dma(dst[r0:r0 + step, :], src[r0:r0 + step, :], writes=[wb_bufs[bn]], q="pool")

        def ld(dst_ap, src_ap, **kw):
            k.dma(dst_ap, src_ap, writes=[B_const], **kw)

        ld(ident_f[:], C["c_ident"])
        ld(maskF_f[:], C["c_maskF"])
        ld(maskB_f[:], C["c_maskB"])
        ld(nmix_fm[:], P["norm_mix_w"].rearrange("(k p) -> p k", p=128), allow_slow_non_contiguous=True)
        ld(nssd_fm[:], P["ssd_norm_w"].rearrange("(k p) -> p k", p=128), allow_slow_non_contiguous=True)
        ld(nffn_fm[:], P["norm_ffn_w"].rearrange("(k p) -> p k", p=128), allow_slow_non_contiguous=True)
        ld(ns5_fm[:], P["s5_norm_w"].rearrange("(k p) -> p k", p=128), allow_slow_non_contiguous=True)
        ld(nfin_bc[:], P["norm_final_w"].rearrange("(o d) -> o d", o=1).to_broadcast([128, D]))
        for t_ in range(5):
            ld(cw_ssd[:, :, t_], P["ssd_conv_w"][t_].rearrange("(k p) -> p k", p=128), allow_slow_non_contiguous=True)
        ld(cb_ssd[:], P["ssd_conv_b"].rearrange("(k p) -> p k", p=128), allow_slow_non_contiguous=True)
        for t_ in range(3):
            ld(cw_ffn[:, :, t_], P["ffn_conv_w"][t_].rearrange("(k p) -> p k", p=128), allow_slow_non_contiguous=True)
        ld(cb_ffn[:], P["ffn_conv_b"].rearrange("(k p) -> p k", p=128), allow_slow_non_contiguous=True)
        ld(a_bc[:, 0:16], P["ssd_a_log_fwd"].rearrange("(o d) -> o d", o=1).to_broadcast([128, 16]))
        ld(a_bc[:, 16:32], P["ssd_a_log_bwd"].rearrange("(o d) -> o d", o=1).to_broadcast([128, 16]))
        ld(dtb_bc[:, 0:16], P["ssd_dt_bias_fwd"].rearrange("(o d) -> o d", o=1).to_broadcast([128, 16]))
        ld(dtb_bc[:, 16:32], P["ssd_dt_bias_bwd"].rearrange("(o d) -> o d", o=1).to_broadcast([128, 16]))
        ld(dsk_bc[:], P["ssd_d"].rearrange("(o d) -> o d", o=1).to_broadcast([128, 16]))
        st["off"] = 0
        esel_f = carve([96, 32 * 128])
        ld(esel_f, C["c_esel"])
        k.op("dve", lambda: V.tensor_copy(out=esel_b[:], in_=esel_f), reads=[B_const], writes=[B_const])
        k.op("dve", lambda: V.tensor_copy(out=ident_b[:], in_=ident_f[:]), reads=[B_const], writes=[B_const])
        k.op("dve", lambda: V.tensor_copy(out=maskF_b[:], in_=maskF_f[:]), reads=[B_const], writes=[B_const])
        k.op("dve", lambda: V.tensor_copy(out=maskB_b[:], in_=maskB_f[:]), reads=[B_const], writes=[B_const])
        k.op("dve", lambda: V.memset(ones_b[:], 1.0), writes=[B_const])
        k.op("dve", lambda: V.memset(eps_t[:], EPS), writes=[B_const])
        k.op("dve", lambda: V.memset(one_t[:], 1.0), writes=[B_const])
        k.op("act", lambda: A_.activation(out=a_bc[:], in_=a_bc[:], func=AF.Exp), reads=[B_const], writes=[B_const])
        k.op("dve", lambda: V.tensor_scalar(out=a_bc[:], in0=a_bc[:], scalar1=-1.0, scalar2=None, op0=ALU.mult),
             reads=[B_const], writes=[B_const])

        B_s5w = Buf('s5w_d')
        B_ys5 = Buf('ys5')
        B_hnfm_d = Buf('hnfm_d')
        blk16_b = sb('blk16_b', [128, 128], BF16)
        g = NS()
        g.__dict__.update({kk: vv for kk, vv in locals().items() if kk != "g"})
        s5_prep(g)
        for b in range(S):
            run_sequence(g, b)
        k.finish()
    return nc


class NS:
    pass


def s5_prep(g):
    nc, k, P, C = g.nc, g.k, g.P, g.C
    V, A_, T_ = nc.vector, nc.scalar, nc.tensor
    carve, pst, PB, sb = g.carve, g.pst, g.PB, g.sb
    ident_f = g.ident_f
    g.arena_reset()
    sel = sb("sel", [128, 4])
    g.gbias = sb("gbias", [128, 32, 2])
    g.lam8r = sb("lam8r", [128, 2, 32])
    g.lam8i = sb("lam8i", [128, 2, 32])
    g.tabr = sb("tabr", [128, 2, 32, 17])
    g.tabi = sb("tabi", [128, 2, 32, 17])
    BP = Buf("s5prep")
    RW = dict(reads=[BP], writes=[BP])
    evec = carve([128, 48])
    evec2 = carve([128, 34])
    mle8 = carve([128, 128])
    mge8 = carve([128, 128])
    blk16 = carve([128, 128])
    Dp = carve([128, 32])
    glu_rep = carve([128, 32, 32])
    Br = carve([128, 32, 16])
    Bi = carve([128, 32, 16])
    for (t, n) in ((sel[:], "c_sel"), (evec, "c_evec"), (evec2, "c_evec2"), (mle8, "c_mle8"), (mge8, "c_mge8"), (blk16, "c_blk16")):
        k.dma(t, C[n], writes=[BP])
    for s in range(8):
        r = slice(16 * s, 16 * s + 16)
        k.dma(Dp[r, :], P["s5_d"].rearrange("(g c) -> c g", c=16), writes=[BP], allow_slow_non_contiguous=True)
        k.dma(glu_rep[r, :, :], P["s5_glu_w"].rearrange("g c d -> c g d"), writes=[BP])
        k.dma(g.gbias[r, :, :], P["s5_glu_b"].rearrange("g (h d) -> d g h", h=2), writes=[BP],
              allow_slow_non_contiguous=True)
    for hf in range(2):
        r = slice(64 * hf, 64 * hf + 64)
        k.dma(Br[r], P["s5_b_re"].rearrange("g p c -> p g c"), writes=[BP])
        k.dma(Bi[r], P["s5_b_im"].rearrange("g p c -> p g c"), writes=[BP])
    k.op("dve", lambda: V.tensor_scalar(out=g.gbias[:, :, 1], in0=g.gbias[:, :, 1], scalar1=0.5, scalar2=None,
                                        op0=ALU.mult), **RW)
    k.op("dve", lambda: V.tensor_copy(out=g.blk16_b[:], in_=blk16), reads=[BP], writes=[g.B_const])

    PW = []
    BB = []
    CC = []
    tA = carve([128, 32, 48])
    tB = carve([128, 32, 48])
    tC = carve([128, 32, 48])
    Cld = carve([128, 4, 64])
    for d, sfx in enumerate(("fwd", "bwd")):
        lamre = carve([128, 32])
        lamim = carve([128, 32])
        step = carve([128, 32])
        for hf in range(2):
            r = slice(64 * hf, 64 * hf + 64)
            k.dma(lamre[r], P["s5_lambda_re_" + sfx].rearrange("g p -> p g"), writes=[BP],
                  allow_slow_non_contiguous=True)
            k.dma(lamim[r], P["s5_lambda_im_" + sfx].rearrange("g p -> p g"), writes=[BP],
                  allow_slow_non_contiguous=True)
        k.dma(step, P["s5_log_step_" + sfx].rearrange("(o g) -> o g", o=1).to_broadcast([128, 32]), writes=[BP])
        Cr = carve([128, 32, 16])
        Ci = carve([128, 32, 16])
        for (dst, nm) in ((Cr, "s5_c_re_" + sfx), (Ci, "s5_c_im_" + sfx)):
            k.dma(Cld, P[nm].rearrange("(t gg) c p -> (gg c) t p", t=4), writes=[BP])
            dflat = dst.rearrange("p g c -> p (g c)")
            for t in range(4):
                k.op("pe", lambda t=t: T_.transpose(out=pst[0][0:64, t * 128:(t + 1) * 128], in_=Cld[:, t, :],
                                                    identity=ident_f[:]), reads=[BP, g.B_const], writes=[PB[0]])
            k.op("dve", lambda: V.tensor_copy(out=dflat[0:64, :], in_=pst[0][0:64, :]), reads=[PB[0]], writes=[BP])
            k.op("dve", lambda: V.tensor_copy(out=dflat[64:128, :], in_=pst[0][0:64, :]), reads=[PB[0]], writes=[BP])
        CC.append((Cr, Ci))
        k.op("act", lambda: A_.activation(out=step, in_=step, func=AF.Exp), **RW)
        sr = carve([128, 32])
        si = carve([128, 32])
        k.op("dve", lambda: V.tensor_mul(out=sr, in0=lamre, in1=step), **RW)
        k.op("dve", lambda: V.tensor_mul(out=si, in0=lamim, in1=step), **RW)
        PWr = carve([128, 32, 48])
        PWi = carve([128, 32, 48])

        def cpow(dR, dI, ev_ap, n):
            tA_, tB_, tC_ = tA[:, :, 0:n], tB[:, :, 0:n], tC[:, :, 0:n]
            ev_b = ev_ap.unsqueeze(1).to_broadcast([128, 32, n])
            k.op("dve", lambda: V.tensor_tensor(out=tA_, in0=sr.unsqueeze(2).to_broadcast([128, 32, n]), in1=ev_b,
                                                op=ALU.mult), **RW)
            k.op("act", lambda: A_.activation(out=tA_, in_=tA_, func=AF.Exp), **RW)
            k.op("dve", lambda: V.tensor_tensor(out=tB_, in0=si.unsqueeze(2).to_broadcast([128, 32, n]), in1=ev_b,
                                                op=ALU.mult), **RW)

            def sin_of(dst, shift):
                if shift != 0.0:
                    k.op("dve", lambda: V.tensor_scalar(out=dst, in0=tB_, scalar1=shift, scalar2=None, op0=ALU.add),
                         **RW)
                    src = dst
                else:
                    src = tB_
                k.op("dve", lambda: V.tensor_scalar(out=tC_, in0=src, scalar1=1.0 / TWO_PI, scalar2=MAGIC,
                                                    op0=ALU.mult, op1=ALU.add), **RW)
                k.op("dve", lambda: V.tensor_scalar(out=tC_, in0=tC_, scalar1=MAGIC, scalar2=-TWO_PI,
                                                    op0=ALU.subtract, op1=ALU.mult), **RW)
                k.op("dve", lambda: V.tensor_tensor(out=dst, in0=src, in1=tC_, op=ALU.add), **RW)
                k.op("act", lambda: A_.activation(out=dst, in_=dst, func=AF.Sin, scale=1.0 - 2e-6), **RW)

            sin_of(dI, 0.0)
            sin_of(dR, 0.5 * math.pi)
            k.op("dve", lambda: V.tensor_mul(out=dR, in0=dR, in1=tA_), **RW)
            k.op("dve", lambda: V.tensor_mul(out=dI, in0=dI, in1=tA_), **RW)

        cpow(PWr, PWi, evec, 48)
        cpow(g.tabr[:, d, :, :], g.tabi[:, d, :, :], evec2[:, 17 * d:17 * d + 17], 17)
        k.op("dve", lambda: V.tensor_scalar(out=g.tabi[:, d, :, :].rearrange("p g e -> p (g e)"),
                                            in0=g.tabi[:, d, :, :].rearrange("p g e -> p (g e)"), scalar1=sel[:, 3:4],
                                            scalar2=None, op0=ALU.mult), **RW)
        PW.append((PWr, PWi))
        k.op("dve", lambda: V.tensor_copy(out=g.lam8r[:, d, :], in_=PWr[:, :, 23]), reads=[BP], writes=[BP])
        k.op("dve", lambda: V.tensor_scalar(out=g.lam8i[:, d, :], in0=PWi[:, :, 23], scalar1=sel[:, 3:4], scalar2=None,
                                            op0=ALU.mult), **RW)
        ar = PWr[:, :, 16]
        ai = PWi[:, :, 16]
        den = carve([128, 32])
        am1 = carve([128, 32])
        cr = carve([128, 32])
        ci = carve([128, 32])
        t1 = carve([128, 32])
        k.op("dve", lambda: V.tensor_mul(out=den, in0=lamre, in1=lamre), **RW)
        k.op("dve", lambda: V.tensor_mul(out=t1, in0=lamim, in1=lamim), **RW)
        k.op("dve", lambda: V.tensor_add(out=den, in0=den, in1=t1), **RW)
        k.op("dve", lambda: V.reciprocal(out=den, in_=den), **RW)
        k.op("dve", lambda: V.tensor_scalar(out=am1, in0=ar, scalar1=-1.0, scalar2=None, op0=ALU.add), **RW)
        k.op("dve", lambda: V.tensor_mul(out=cr, in0=am1, in1=lamre), **RW)
        k.op("dve", lambda: V.tensor_mul(out=t1, in0=ai, in1=lamim), **RW)
        k.op("dve", lambda: V.tensor_add(out=cr, in0=cr, in1=t1), **RW)
        k.op("dve", lambda: V.tensor_mul(out=cr, in0=cr, in1=den), **RW)
        k.op("dve", lambda: V.tensor_mul(out=ci, in0=ai, in1=lamre), **RW)
        k.op("dve", lambda: V.tensor_mul(out=t1, in0=am1, in1=lamim), **RW)
        k.op("dve", lambda: V.tensor_sub(out=ci, in0=ci, in1=t1), **RW)
        k.op("dve", lambda: V.tensor_mul(out=ci, in0=ci, in1=den), **RW)
        Bbr = carve([128, 32, 16])
        Bbi = carve([128, 32, 16])
        t3 = carve([128, 32, 16])
        crb = cr.unsqueeze(2).to_broadcast([128, 32, 16])
        cib = ci.unsqueeze(2).to_broadcast([128, 32, 16])
        k.op("dve", lambda: V.tensor_tensor(out=Bbr, in0=Br, in1=crb, op=ALU.mult), **RW)
        k.op("dve", lambda: V.tensor_tensor(out=t3, in0=Bi, in1=cib, op=ALU.mult), **RW)
        k.op("dve", lambda: V.tensor_sub(out=Bbr, in0=Bbr, in1=t3), **RW)
        k.op("dve", lambda: V.tensor_tensor(out=Bbi, in0=Bi, in1=crb, op=ALU.mult), **RW)
        k.op("dve", lambda: V.tensor_tensor(out=t3, in0=Br, in1=cib, op=ALU.mult), **RW)
        k.op("dve", lambda: V.tensor_add(out=Bbi, in0=Bbi, in1=t3), **RW)
        BB.append((Bbr, Bbi))

    GH = 16
    RE = carve([128, GH, 8, 16])
    IM = carve([128, GH, 8, 16])
    T1 = carve([128, GH, 8, 16])
    TAB = [carve([128, GH, 8, 16]) for _ in range(4)]
    stage = carve([128, GH, 7, 128], BF16)
    tq = carve([128, 4, 128])
    B_stage = Buf("s5stage")

    def table(out, d, idx, Mr, Mi, selcol, g0):
        PWr, PWi = PW[d]
        pr = PWr[:, g0:g0 + GH, idx * 8:(idx + 1) * 8].unsqueeze(3).to_broadcast([128, GH, 8, 16])
        pi = PWi[:, g0:g0 + GH, idx * 8:(idx + 1) * 8].unsqueeze(3).to_broadcast([128, GH, 8, 16])
        mr = Mr[:, g0:g0 + GH, :].unsqueeze(2).to_broadcast([128, GH, 8, 16])
        mi = Mi[:, g0:g0 + GH, :].unsqueeze(2).to_broadcast([128, GH, 8, 16])
        k.op("dve", lambda: V.tensor_tensor(out=RE, in0=mr, in1=pr, op=ALU.mult), **RW)
        k.op("dve", lambda: V.tensor_tensor(out=T1, in0=mi, in1=pi, op=ALU.mult), **RW)
        k.op("dve", lambda: V.tensor_sub(out=RE, in0=RE, in1=T1), **RW)
        k.op("dve", lambda: V.tensor_tensor(out=IM, in0=mr, in1=pi, op=ALU.mult), **RW)
        k.op("dve", lambda: V.tensor_tensor(out=T1, in0=mi, in1=pr, op=ALU.mult), **RW)
        k.op("dve", lambda: V.tensor_add(out=IM, in0=IM, in1=T1), **RW)
        fl = lambda a: a.rearrange("p g s c -> p (g s c)")
        k.op("dve", lambda: V.tensor_scalar(out=fl(RE), in0=fl(RE), scalar1=sel[:, 0:1], scalar2=None, op0=ALU.mult),
             **RW)
        k.op("dve", lambda: V.scalar_tensor_tensor(out=fl(out), in0=fl(IM), scalar=sel[:, selcol:selcol + 1],
                                                   in1=fl(RE), op0=ALU.mult, op1=ALU.add),
             reads=[BP, B_stage], writes=[BP, B_stage])

    for g0 in (0, 16):
        for d in range(2):
            Bbr, Bbi = BB[d]
            Cr, Ci = CC[d]
            table(TAB[2 * d], d, 3 * d + 0, Bbr, Bbi, 1, g0)
            table(TAB[2 * d + 1], d, 3 * d + 1, Cr, Ci, 2, g0)
            table(T1, d, 3 * d + 2, Cr, Ci, 2, g0)
            k.op("dve", lambda d=d: V.tensor_copy(out=stage[:, :, 3 + d, :],
                                                  in_=T1.rearrange("p g s c -> p g (s c)")),
                 reads=[BP], writes=[B_stage])
        for gq in range(0, GH, 4):
            for d in range(2):
                for j in range(4):
                    k.op("pe", lambda d=d, j=j: T_.transpose(
                        out=pst[1 + d][:, j * 128:(j + 1) * 128],
                        in_=TAB[2 * d][:, gq + j].rearrange("p s c -> p (s c)"), identity=ident_f[:]),
                        reads=[BP, B_stage], writes=[PB[1 + d]])
                k.op("act", lambda d=d: A_.copy(out=stage[:, gq:gq + 4, 1 + d, :],
                                                in_=pst[1 + d][:, :].rearrange("p (g n) -> p g n", g=4)),
                     reads=[PB[1 + d]], writes=[B_stage])
            for d in range(2):
                for j in range(4):
                    k.op("pe", lambda d=d, j=j: T_.matmul(
                        pst[3 + d][:, j * 128:(j + 1) * 128],
                        lhsT=TAB[2 * d][:, gq + j].rearrange("p s c -> p (s c)"),
                        rhs=TAB[2 * d + 1][:, gq + j].rearrange("p s c -> p (s c)"), start=True, stop=True),
                        reads=[BP, B_stage], writes=[PB[3 + d]])
            k.op("dve", lambda: V.tensor_tensor(out=tq, in0=pst[3][:, :].rearrange("p (g n) -> p g n", g=4),
                                                in1=mle8.unsqueeze(1).to_broadcast([128, 4, 128]), op=ALU.mult),
                 reads=[PB[3], BP], writes=[BP])
            tq2 = RE.rearrange("p g s c -> p (g s c)")[:, 0:512].rearrange("p (g n) -> p g n", g=4)
            k.op("dve", lambda: V.tensor_tensor(out=tq2, in0=pst[4][:, :].rearrange("p (g n) -> p g n", g=4),
                                                in1=mge8.unsqueeze(1).to_broadcast([128, 4, 128]), op=ALU.mult),
                 reads=[PB[4], BP], writes=[BP])
            k.op("dve", lambda: V.tensor_add(out=tq, in0=tq, in1=tq2), **RW)
            for j in range(4):
                gg = g0 + gq + j
                k.op("dve", lambda j=j, gg=gg: V.scalar_tensor_tensor(
                    out=tq[:, j, :], in0=ident_f[:], scalar=Dp[:, gg:gg + 1], in1=tq[:, j, :],
                    op0=ALU.mult, op1=ALU.add), reads=[BP, g.B_const], writes=[BP])
            k.op("dve", lambda: V.tensor_copy(out=stage[:, gq:gq + 4, 0, :], in_=tq), reads=[BP], writes=[B_stage])
        for h in range(2):
            gsrc = glu_rep[:, g0:g0 + GH, h * 16:(h + 1) * 16].unsqueeze(2).to_broadcast([128, GH, 8, 16])
            bsrc = blk16.rearrange("p (q d) -> p q d", q=8).unsqueeze(1).to_broadcast([128, GH, 8, 16])
            k.op("dve", lambda: V.tensor_tensor(out=RE, in0=gsrc, in1=bsrc, op=ALU.mult), **RW)
            k.op("dve", lambda h=h: V.tensor_scalar(out=stage[:, :, 5 + h, :],
                                                    in0=RE.rearrange("p g s c -> p g (s c)"), scalar1=0.5,
                                                    scalar2=None, op0=ALU.mult), reads=[BP], writes=[B_stage])
        k.dma(g.s5w_d[:, g0 * 896:(g0 + GH) * 896], stage.rearrange("p g k n -> p (g k n)"), reads=[B_stage],
              writes=[g.B_s5w])


def pipeline(n, stages, name=""):
    import os
    if os.environ.get("NOPIPE") or (os.environ.get("PIPE") is not None and name not in os.environ["PIPE"].split(",")):
        for m in range(n):
            for sk, fn in sorted(stages, key=lambda z: z[0]):
                fn(m)
        return
    mx = max(sk for sk, _ in stages)
    for t in range(n + mx):
        for sk, fn in sorted(stages, key=lambda z: -z[0]):
            m = t - sk
            if 0 <= m < n:
                fn(m)


def dump(g, b, name, ap, reads):
    if b == 0 and name in g.dbg_out:
        g.k.dma(g.dbg_out[name], ap, reads=reads, writes=[Buf("dbg")])


def run_sequence(g, b):
    phase_a(g, b)
    if g.stop_after == "a":
        return
    phase_b(g, b)
    if g.stop_after == "b":
        return
    phase_d(g, b)


def phase_a(g, b):
    nc, k, P = g.nc, g.k, g.P
    V, A_, T_ = nc.vector, nc.scalar, nc.tensor
    carve, pst, PB = g.carve, g.pst, g.PB
    g.arena_reset()
    OGU = carve([128, 4, L], BF16)
    B_ogu = Buf("ogu")
    s5w = carve([128, 32, 7, 128], BF16)
    B_w = Buf("s5w")
    for h in range(2):
        k.dma(s5w[:, 16 * h:16 * h + 16].rearrange("p g k n -> p (g k n)"),
              g.s5w_d[:, 16 * h * 896:(16 * h + 16) * 896], reads=[g.B_s5w], writes=[B_w])
    hn_fm = carve([128, 8, L], BF16)
    B_hn = Buf("hn_fm")
    Wu = carve([128, 8, 512], BF16)
    B_Wu = Buf("Wu")
    xt = [carve([128, D]) for _ in range(2)]
    B_xt = [Buf("xt0"), Buf("xt1")]
    junk = carve([128, D], BF16)
    B_junk = Buf("junk")
    hnb = [carve([128, D], BF16) for _ in range(2)]
    B_hnb = [Buf("hnb0"), Buf("hnb1")]
    stat = [carve([128, 4]) for _ in range(2)]
    B_stat = [Buf("st0"), Buf("st1")]
    BC = g.B_const

    k.dma(Wu, g.win_b[:, 3104:3616].rearrange("(kt p) n -> p kt n", p=128), reads=[g.wb_bufs["win"]], writes=[B_Wu])
    def p0a(c):
        i = c % 2
        k.dma(xt[i], g.x[b, c * 128:(c + 1) * 128, :], writes=[B_xt[i]])
        k.op("act", lambda: A_.activation(out=junk, in_=xt[i], func=AF.Square, accum_out=stat[i][:, 0:1]),
             reads=[B_xt[i]], writes=[B_junk, B_stat[i]])
        k.op("act", lambda: A_.activation(out=stat[i][:, 1:2], in_=stat[i][:, 0:1], func=AF.Ln, scale=1.0 / D,
                                          bias=g.eps_t[:]), reads=[B_stat[i], BC], writes=[B_stat[i]])
        k.op("act", lambda: A_.activation(out=stat[i][:, 2:3], in_=stat[i][:, 1:2], func=AF.Exp, scale=-0.5),
             reads=[B_stat[i]], writes=[B_stat[i]])
        k.op("dve", lambda: V.tensor_scalar(out=hnb[i], in0=xt[i], scalar1=stat[i][:, 2:3], scalar2=None,
                                            op0=ALU.mult), reads=[B_xt[i], B_stat[i]], writes=[B_hnb[i]])

    def p0b(c):
        i = c % 2
        psb = pst[i][:, :].bitcast(BF16)

        def tr():
            ins = None
            for kt in range(8):
                ins = T_.transpose(out=psb[:, kt * 128:(kt + 1) * 128], in_=hnb[i][:, kt * 128:(kt + 1) * 128],
                                   identity=g.ident_b[:])
            return ins
        k.op("pe", tr, reads=[B_hnb[i], BC], writes=[PB[i]])
        k.op("dve", lambda: V.tensor_tensor(out=hn_fm[:, :, c * 128:(c + 1) * 128],
                                            in0=psb.rearrange("p (k t) -> p k t", k=8),
                                            in1=g.nmix_fm[:].unsqueeze(2).to_broadcast([128, 8, 128]), op=ALU.mult),
             reads=[PB[i], BC], writes=[B_hn])

    pipeline(NCH, [(0, p0a), (1, p0b)], "p0")
    for kt in range(8):
        k.dma(g.hnfm_d[kt * 128:(kt + 1) * 128, :], hn_fm[:, kt, :], reads=[B_hn], writes=[g.B_hnfm_d])
    if b == 0 and "d_hn" in g.dbg_out:
        dtmp = carve([128, 8, L])
        Bd = Buf("dtmp")
        k.op("dve", lambda: V.tensor_copy(out=dtmp, in_=hn_fm), reads=[B_hn], writes=[Bd])
        dump(g, b, "d_hn", dtmp.rearrange("p k t -> p (k t)"), [Bd])

    for Tt in range(4):
        for nt in range(4):
            pi = 2 + (Tt * 4 + nt) % 2

            def mm():
                ins = None
                for kt in range(8):
                    ins = T_.matmul(pst[pi][:, :], lhsT=Wu[:, kt, Tt * 128:(Tt + 1) * 128],
                                    rhs=hn_fm[:, kt, nt * 512:(nt + 1) * 512], start=(kt == 0), stop=(kt == 7))
                return ins
            k.op("pe", mm, reads=[B_Wu, B_hn], writes=[PB[pi]])
            k.op("act", lambda: A_.copy(
                out=OGU[:, Tt, :].rearrange("p (s j) -> p s j", s=8)[:, :, nt * 64:(nt + 1) * 64],
                in_=pst[pi][:, :].rearrange("p (j s) -> p s j", s=8)), reads=[PB[pi]], writes=[B_ogu])

    g.arena_reset(keep=4 * L * 2 + 32 * 7 * 128 * 2)
    U = carve([128, 32, 256], BF16)
    B_U = Buf("U")
    SH = carve([128, 2, 32, 256], BF16)
    B_SH = [Buf("SH0"), Buf("SH1")]
    if not hasattr(g, "B_usd"):
        g.B_usd, g.B_ysd = Buf("us_d"), Buf("ys_d")
    k.dma(g.us_d.rearrange("(t p) n -> p t n", p=128), OGU, reads=[B_ogu], writes=[g.B_usd])
    for gi in range(32):
        k.dma(U[:, gi, :], g.us_d[gi * 16:(gi + 1) * 16, :].rearrange("c (s j) -> s c j", s=8),
              reads=[g.B_usd], writes=[B_U])
    for gi in range(32):
        pi = 2 + gi % 2

        def mm():
            ins = None
            for d in range(2):
                ins = T_.matmul(pst[pi][:, d * 256:(d + 1) * 256], lhsT=s5w[:, gi, 1 + d, :], rhs=U[:, gi, :],
                                start=True, stop=True)
            return ins
        k.op("pe", mm, reads=[B_w, B_U], writes=[PB[pi]])
        k.op("act", lambda: A_.copy(out=SH[:, :, gi, :], in_=pst[pi][:, :].rearrange("p (d j) -> p d j", d=2)),
             reads=[PB[pi]], writes=B_SH)
    rec_off = g.st["off"]
    for d in range(2):
        E = "dve" if d == 0 else "pool"
        EN = V if d == 0 else nc.gpsimd
        sh3 = [128, 32, 16]
        Xa, Xb, Xs, T1, T2, Hin, HinS = (carve(sh3) for _ in range(7))
        Ya, Yb, Ys, U1, U2 = (carve([128, 32]) for _ in range(5))
        C1 = carve([128, 2, 16, 16])
        C2 = carve([128, 2, 16, 16])
        bX = {id(Xa): Buf("Xa%d" % d), id(Xb): Buf("Xb%d" % d)}
        bXs, bT1, bT2, bHin, bC = Buf("Xs"), Buf("T1"), Buf("T2"), Buf("Hin"), Buf("C12")
        bY = {id(Ya): Buf("Ya"), id(Yb): Buf("Yb")}
        bYs, bU1, bU2 = Buf("Ys"), Buf("U1"), Buf("U2")
        SHv = SH[:, d].rearrange("p g (J i) -> p g J i", i=16)
        Ar = g.lam8r[:, d, :].unsqueeze(2).to_broadcast(sh3)
        Ai = g.lam8i[:, d, :].unsqueeze(2).to_broadcast(sh3)
        k.op(E, lambda: EN.memset(Xa, 0.0), writes=[bX[id(Xa)]])
        X, Xn = Xa, Xb
        for t in range(16):
            i = t if d == 0 else 15 - t
            bx, bxn = bX[id(X)], bX[id(Xn)]
            k.op(E, lambda: EN.tensor_copy(out=Xs[0:64], in_=X[64:128]), reads=[bx], writes=[bXs])
            k.op(E, lambda: EN.tensor_copy(out=Xs[64:128], in_=X[0:64]), reads=[bx], writes=[bXs])
            k.op(E, lambda: EN.tensor_tensor(out=T1, in0=X, in1=Ar, op=ALU.mult), reads=[bx, BC], writes=[bT1])
            k.op(E, lambda: EN.tensor_tensor(out=T2, in0=Xs, in1=Ai, op=ALU.mult), reads=[bXs, BC], writes=[bT2])
            k.op(E, lambda: EN.tensor_tensor(out=T1, in0=T1, in1=T2, op=ALU.add), reads=[bT1, bT2], writes=[bT1])
            k.op(E, lambda: EN.tensor_tensor(out=Xn, in0=T1, in1=SHv[:, :, :, i], op=ALU.add),
                 reads=[bT1, B_SH[d]], writes=[bxn])
            k.op(E, lambda: EN.tensor_copy(out=SHv[:, :, :, i], in_=X), reads=[bx], writes=[B_SH[d]])
            X, Xn = Xn, X
        bE = bX[id(X)]
        A128r = g.tabr[:, d, :, 16]
        A128i = g.tabi[:, d, :, 16]
        k.op(E, lambda: EN.memset(Ya, 0.0), writes=[bY[id(Ya)]])
        Y, Yn = Ya, Yb
        for t in range(16):
            J = t if d == 0 else 15 - t
            by, byn = bY[id(Y)], bY[id(Yn)]
            k.op(E, lambda: EN.tensor_copy(out=Hin[:, :, J], in_=Y), reads=[by], writes=[bHin])
            k.op(E, lambda: EN.tensor_copy(out=Ys[0:64], in_=Y[64:128]), reads=[by], writes=[bYs])
            k.op(E, lambda: EN.tensor_copy(out=Ys[64:128], in_=Y[0:64]), reads=[by], writes=[bYs])
            k.op(E, lambda: EN.tensor_tensor(out=U1, in0=Y, in1=A128r, op=ALU.mult), reads=[by, BC], writes=[bU1])
            k.op(E, lambda: EN.tensor_tensor(out=U2, in0=Ys, in1=A128i, op=ALU.mult), reads=[bYs, BC], writes=[bU2])
            k.op(E, lambda: EN.tensor_tensor(out=U1, in0=U1, in1=U2, op=ALU.add), reads=[bU1, bU2], writes=[bU1])
            k.op(E, lambda: EN.tensor_tensor(out=Yn, in0=U1, in1=X[:, :, J], op=ALU.add), reads=[bU1, bE],
                 writes=[byn])
            Y, Yn = Yn, Y
        k.op(E, lambda: EN.tensor_copy(out=HinS[0:64], in_=Hin[64:128]), reads=[bHin], writes=[bHin])
        k.op(E, lambda: EN.tensor_copy(out=HinS[64:128], in_=Hin[0:64]), reads=[bHin], writes=[bHin])
        for gs in range(16):
            gsl = slice(2 * gs, 2 * gs + 2)
            sh4 = [128, 2, 16, 16]
            tr_b = g.tabr[:, d, gsl, 0:16].unsqueeze(2).to_broadcast(sh4)
            ti_b = g.tabi[:, d, gsl, 0:16].unsqueeze(2).to_broadcast(sh4)
            k.op(E, lambda: EN.tensor_tensor(out=C1, in0=Hin[:, gsl, :].unsqueeze(3).to_broadcast(sh4), in1=tr_b,
                                             op=ALU.mult), reads=[bHin, BC, bC], writes=[bC])
            k.op(E, lambda: EN.tensor_tensor(out=C2, in0=HinS[:, gsl, :].unsqueeze(3).to_broadcast(sh4), in1=ti_b,
                                             op=ALU.mult), reads=[bHin, BC, bC], writes=[bC])
            k.op(E, lambda: EN.tensor_tensor(out=C1, in0=C1, in1=C2, op=ALU.add), reads=[bC], writes=[bC])
            k.op(E, lambda: EN.tensor_tensor(out=SHv[:, gsl], in0=SHv[:, gsl], in1=C1, op=ALU.add),
                 reads=[bC, B_SH[d]], writes=[B_SH[d]])
    g.arena_reset(keep=rec_off)
    ysb = [carve([128, 256]) for _ in range(2)]
    y2 = [carve([128, 256]) for _ in range(2)]
    gbf = [carve([128, 256], BF16) for _ in range(2)]
    sg = [carve([128, 256]) for _ in range(2)]
    sqb = [carve([128, 256], BF16) for _ in range(2)]
    B_y = [Buf("ysb0"), Buf("ysb1")]
    B_y2 = [Buf("y20"), Buf("y21")]
    B_g = [Buf("gbf0"), Buf("gbf1")]
    B_sg = [Buf("sg0"), Buf("sg1")]
    B_sq = [Buf("sq0"), Buf("sq1")]
    OG = OGU.rearrange("p t n -> p (t n)").rearrange("p (g j) -> p g j", g=32)
    for gi in range(32):
        i = gi % 2
        py = 4 + i
        pg = 2 + i

        def mmy():
            T_.matmul(pst[py][:, 0:256], lhsT=s5w[:, gi, 0, :], rhs=U[:, gi, :], start=True, stop=False)
            T_.matmul(pst[py][:, 0:256], lhsT=s5w[:, gi, 3, :], rhs=SH[:, 0, gi, :], start=False, stop=False)
            return T_.matmul(pst[py][:, 0:256], lhsT=s5w[:, gi, 4, :], rhs=SH[:, 1, gi, :], start=False, stop=True)
        k.op("pe", mmy, reads=[B_w, B_U] + B_SH, writes=[PB[py]])
        k.op("act", lambda: A_.copy(out=ysb[i], in_=pst[py][:, 0:256]), reads=[PB[py]], writes=[B_y[i]])
        if gi == 0:
            dump(g, b, "d_y5", ysb[i], [B_y[i]])
        k.op("dve", lambda: V.tensor_tensor(out=y2[i], in0=ysb[i], in1=ysb[i], op=ALU.mult), reads=[B_y[i]],
             writes=[B_y2[i]])
        k.op("dve", lambda: V.tensor_scalar(out=y2[i], in0=y2[i], scalar1=0.044715, scalar2=1.0, op0=ALU.mult,
                                            op1=ALU.add), reads=[B_y2[i]], writes=[B_y2[i]])
        k.op("dve", lambda: V.tensor_tensor(out=y2[i], in0=y2[i], in1=ysb[i], op=ALU.mult), reads=[B_y[i], B_y2[i]],
             writes=[B_y2[i]])
        k.op("act", lambda: A_.activation(out=y2[i], in_=y2[i], func=AF.Tanh, scale=0.7978845608028654),
             reads=[B_y2[i]], writes=[B_y2[i]])
        k.op("dve", lambda: V.scalar_tensor_tensor(out=gbf[i], in0=y2[i], scalar=1.0, in1=ysb[i], op0=ALU.add,
                                                   op1=ALU.mult), reads=[B_y2[i], B_y[i]], writes=[B_g[i]])

        def mmg():
            T_.matmul(pst[pg][:, 0:256], lhsT=s5w[:, gi, 5, :], rhs=gbf[i], start=True, stop=True)
            return T_.matmul(pst[pg][:, 256:512], lhsT=s5w[:, gi, 6, :], rhs=gbf[i], start=True, stop=True)
        k.op("pe", mmg, reads=[B_w, B_g[i]], writes=[PB[pg]])
        k.op("act", lambda: A_.activation(out=sg[i], in_=pst[pg][:, 256:512], func=AF.Tanh, scale=0.5,
                                          bias=g.gbias[:, gi, 1:2]), reads=[PB[pg], BC], writes=[B_sg[i]])
        k.op("dve", lambda: V.tensor_scalar(out=sg[i], in0=sg[i], scalar1=0.5, scalar2=0.5, op0=ALU.mult,
                                            op1=ALU.add), reads=[B_sg[i]], writes=[B_sg[i]])
        k.op("dve", lambda: V.scalar_tensor_tensor(out=OG[:, gi, :], in0=pst[pg][:, 0:256],
                                                   scalar=g.gbias[:, gi, 0:1], in1=sg[i], op0=ALU.add, op1=ALU.mult),
             reads=[PB[pg], B_sg[i], BC], writes=[B_ogu])
        k.op("act", lambda: A_.activation(out=sqb[i], in_=OG[:, gi, :], func=AF.Square), reads=[B_ogu],
             writes=[B_sq[i]])
        k.op("pe", lambda: T_.matmul(pst[6][:, 0:256], lhsT=g.blk16_b[:], rhs=sqb[i], start=(gi == 0),
                                     stop=(gi == 31)), reads=[B_sq[i], BC], writes=[PB[6]])
    rs = carve([128, 256])
    B_rs = Buf("rs")
    k.op("act", lambda: A_.activation(out=rs, in_=pst[6][:, 0:256], func=AF.Ln, scale=1.0 / 512, bias=g.eps_t[:]),
         reads=[PB[6], BC], writes=[B_rs])
    k.op("act", lambda: A_.activation(out=rs, in_=rs, func=AF.Exp, scale=-0.5), reads=[B_rs], writes=[B_rs])
    rsb = carve([128, 256], BF16)
    k.op("dve", lambda: V.tensor_copy(out=rsb, in_=rs), reads=[B_rs], writes=[B_rs])
    k.op("dve", lambda: V.tensor_tensor(out=OG, in0=OG, in1=rsb.unsqueeze(1).to_broadcast([128, 32, 256]),
                                        op=ALU.mult), reads=[B_ogu, B_rs], writes=[B_ogu])
    Yp = U.rearrange("p g j -> p (g j)").rearrange("p (t q j) -> p t q j", t=4, q=8)
    for gi in range(32):
        k.dma(g.ys_d[gi * 16:(gi + 1) * 16, :].rearrange("d (q j) -> q d j", q=8), OG[:, gi, :],
              reads=[B_ogu], writes=[g.B_ysd])
    k.dma(U.rearrange("p g j -> p (g j)").rearrange("p (t n) -> p t n", t=4),
          g.ys_d.rearrange("(t p) n -> p t n", p=128), reads=[g.B_ysd], writes=[B_U])
    for Tt in range(4):
        k.op("dve", lambda: V.tensor_scalar(out=g.ys5_fm[:, Tt, :].rearrange("p (j q) -> p q j", q=8),
                                            in0=Yp[:, Tt, :, :], scalar1=g.ns5_fm[:, Tt:Tt + 1], scalar2=None,
                                            op0=ALU.mult), reads=[B_U, BC], writes=[g.B_ys5])
    if b == 0 and "d_ys5" in g.dbg_out:
        dtmp = carve([128, 4, L])
        Bd = Buf("dtmp")
        k.op("dve", lambda: V.tensor_copy(out=dtmp, in_=g.ys5_fm[:]), reads=[g.B_ys5], writes=[Bd])
        dump(g, b, "d_ys5", dtmp.rearrange("p k t -> p (k t)"), [Bd])


def _rmsnorm_stats(g, src, stat, junk, rd, B_stat, B_junk, scale):
    k, A_ = g.k, g.nc.scalar
    k.op("act", lambda: A_.activation(out=junk, in_=src, func=AF.Square, accum_out=stat[:, 0:1]),
         reads=rd, writes=[B_junk, B_stat])
    k.op("act", lambda: A_.activation(out=stat[:, 1:2], in_=stat[:, 0:1], func=AF.Ln, scale=scale, bias=g.eps_t[:]),
         reads=[B_stat, g.B_const], writes=[B_stat])
    k.op("act", lambda: A_.activation(out=stat[:, 2:3], in_=stat[:, 1:2], func=AF.Exp, scale=-0.5),
         reads=[B_stat], writes=[B_stat])


def phase_b(g, b):
    nc, k, P = g.nc, g.k, g.P
    V, A_, T_ = nc.vector, nc.scalar, nc.tensor
    carve, pst, PB, BC = g.carve, g.pst, g.PB, g.B_const
    if not hasattr(g, "B_dram"):
        g.B_dram = {n: Buf(n) for n in ("xs", "btm", "bfm", "cfm", "zs", "h1", "hn2", "g")}
    BD = g.B_dram
    g.arena_reset()
    dt_t = carve([128, NCH, 32])
    dta = carve([128, NCH, 32])
    rtmp = carve([128, NCH, 32])
    dsp = [carve([128, NCH, 32], BF16) for _ in range(3)]
    B_dt = Buf("dt")
    keep = g.st["off"]

    hn_fm = carve([128, 8, L], BF16)
    B_hn = Buf("hn_fm_b")
    for kt in range(8):
        k.dma(hn_fm[:, kt, :], g.hnfm_d[kt * 128:(kt + 1) * 128, :], reads=[g.B_hnfm_d], writes=[B_hn])
    Wz = carve([128, 8, D], BF16)
    Wdt = carve([128, 8, 32], BF16)
    B_Wz = Buf("Wz")
    k.dma(Wz, g.win_b[:, 0:D].rearrange("(kt p) n -> p kt n", p=128), reads=[g.wb_bufs["win"]], writes=[B_Wz])
    k.dma(Wdt, g.win_b[:, 3072:3104].rearrange("(kt p) n -> p kt n", p=128), reads=[g.wb_bufs["win"]], writes=[B_Wz])
    wt = [carve([128, 8, 128], BF16) for _ in range(2)]
    B_wt = [Buf("wt0"), Buf("wt1")]
    pre = [carve([128, L + 4]) for _ in range(2)]
    B_pre = [Buf("pre0"), Buf("pre1")]
    acc = [carve([128, L]) for _ in range(2)]
    B_acc = [Buf("acc0"), Buf("acc1")]
    xc = [carve([128, L], BF16) for _ in range(2)]
    B_xc = [Buf("xc0"), Buf("xc1")]
    xst = [carve([128, NCH, 128], BF16) for _ in range(2)]
    B_xst = [Buf("xst0"), Buf("xst1")]
    zst = [carve([128, D], BF16) for _ in range(2)]
    B_zst = [Buf("zst0"), Buf("zst1")]
    for i in range(2):
        k.op("dve", lambda: V.memset(pre[i][:, 0:2], 0.0), writes=[B_pre[i]])
        k.op("dve", lambda: V.memset(pre[i][:, L + 2:L + 4], 0.0), writes=[B_pre[i]])
    cw, cb = g.cw_ssd, g.cb_ssd

    def wload(m):
        k.dma(wt[m % 2], g.win_b[:, D + m * 128:D + (m + 1) * 128].rearrange("(kt p) n -> p kt n", p=128),
              reads=[g.wb_bufs["win"]], writes=[B_wt[m % 2]])

    def xs1(m):
        i = m % 2
        if m == 0:
            wload(0)
        if m + 1 < 16:
            wload(m + 1)
        for nt in range(4):
            pi = 2 + nt % 2

            def mm():
                ins = None
                for kt in range(8):
                    ins = T_.matmul(pst[pi][:, :], lhsT=wt[i][:, kt, :], rhs=hn_fm[:, kt, nt * 512:(nt + 1) * 512],
                                    start=(kt == 0), stop=(kt == 7))
                return ins
            k.op("pe", mm, reads=[B_wt[i], B_hn], writes=[PB[pi]])
            k.op("act", lambda: A_.copy(out=pre[i][:, 2 + nt * 512:2 + (nt + 1) * 512], in_=pst[pi][:, :]),
                 reads=[PB[pi]], writes=[B_pre[i]])

    def xs2(m):
        i = m % 2
        k.op("act", lambda: A_.activation(out=acc[i], in_=pre[i][:, 2:L + 2], func=AF.Identity, scale=cw[:, m, 2:3],
                                          bias=cb[:, m:m + 1]), reads=[B_pre[i], BC], writes=[B_acc[i]])
        for tap in (0, 1, 3, 4):
            k.op("dve", lambda: V.scalar_tensor_tensor(out=acc[i], in0=pre[i][:, tap:tap + L],
                                                       scalar=cw[:, m, tap:tap + 1], in1=acc[i], op0=ALU.mult,
                                                       op1=ALU.add), reads=[B_pre[i], B_acc[i], BC],
                 writes=[B_acc[i]])

    def xs3(m):
        i = m % 2
        k.op("act", lambda: A_.activation(out=xc[i], in_=acc[i], func=AF.Silu), reads=[B_acc[i]], writes=[B_xc[i]])
        if m >= 8:
            dst, bn = (g.bfm_d, "bfm") if m < 12 else (g.cfm_d, "cfm")
            r0 = (m - 8) % 4 * 128
            k.dma(dst[r0:r0 + 128, :], xc[i], reads=[B_xc[i]], writes=[BD[bn]])
        if m < 12:
            for half in range(2):
                pb = 4 + half
                psb = pst[pb][:, :].bitcast(BF16)

                def tr():
                    ins = None
                    for cc in range(8):
                        c = half * 8 + cc
                        ins = T_.transpose(out=psb[:, cc * 128:(cc + 1) * 128], in_=xc[i][:, c * 128:(c + 1) * 128],
                                           identity=g.ident_b[:])
                    return ins
                k.op("pe", tr, reads=[B_xc[i], BC], writes=[PB[pb]])
                k.op("dve", lambda: V.tensor_copy(out=xst[i][:, half * 8:(half + 1) * 8, :],
                                                  in_=psb.rearrange("p (c t) -> p c t", c=8)),
                     reads=[PB[pb]], writes=[B_xst[i]])
            if m < 8:
                k.dma(g.xs_d.rearrange("(c t) d -> t c d", t=128)[:, :, m * 128:(m + 1) * 128], xst[i],
                      reads=[B_xst[i]], writes=[BD["xs"]])
            else:
                k.dma(g.btm_d.rearrange("(c t) d -> t c d", t=128)[:, :, (m - 8) * 128:(m - 7) * 128], xst[i],
                      reads=[B_xst[i]], writes=[BD["btm"]])

    pipeline(16, [(0, xs1), (1, xs2), (2, xs3)], "b1")
    for c in range(NCH):
        i = c % 2

        def mmz():
            ins = None
            for half in range(2):
                for kt in range(8):
                    ins = T_.matmul(pst[6 + half][:, :], lhsT=hn_fm[:, kt, c * 128:(c + 1) * 128],
                                    rhs=Wz[:, kt, half * 512:(half + 1) * 512], start=(kt == 0), stop=(kt == 7))
            return ins
        k.op("pe", mmz, reads=[B_hn, B_Wz], writes=[PB[6], PB[7]])
        for half in range(2):
            k.op("act", lambda: A_.activation(out=zst[i][:, half * 512:(half + 1) * 512], in_=pst[6 + half][:, :],
                                              func=AF.Silu), reads=[PB[6 + half]], writes=[B_zst[i]])
        k.dma(g.zs_d[c * 128:(c + 1) * 128, :], zst[i], reads=[B_zst[i]], writes=[BD["zs"]])

        def mmd():
            ins = None
            for kt in range(8):
                ins = T_.matmul(pst[1][:, 0:32], lhsT=hn_fm[:, kt, c * 128:(c + 1) * 128], rhs=Wdt[:, kt, :],
                                start=(kt == 0), stop=(kt == 7))
            return ins
        k.op("pe", mmd, reads=[B_hn, B_Wz], writes=[PB[1]])
        k.op("dve", lambda: V.tensor_tensor(out=dt_t[:, c, :], in0=pst[1][:, 0:32], in1=g.dtb_bc[:], op=ALU.add),
             reads=[PB[1], BC], writes=[B_dt])
    RWd = dict(reads=[B_dt, BC], writes=[B_dt])
    k.op("act", lambda: A_.activation(out=dt_t, in_=dt_t, func=AF.Exp), **RWd)
    k.op("act", lambda: A_.activation(out=dt_t, in_=dt_t, func=AF.Ln, bias=g.one_t[:]), **RWd)
    k.op("dve", lambda: V.tensor_tensor(out=dta, in0=dt_t, in1=g.a_bc[:].unsqueeze(1).to_broadcast([128, NCH, 32]),
                                        op=ALU.mult), **RWd)
    k.op("dve", lambda: V.tensor_copy(out=dsp[0], in_=dta), **RWd)
    k.op("dve", lambda: V.tensor_tensor(out=rtmp, in0=dta, in1=dsp[0], op=ALU.subtract), **RWd)
    k.op("dve", lambda: V.tensor_copy(out=dsp[1], in_=rtmp), **RWd)
    k.op("dve", lambda: V.tensor_tensor(out=rtmp, in0=rtmp, in1=dsp[1], op=ALU.subtract), **RWd)
    k.op("dve", lambda: V.tensor_copy(out=dsp[2], in_=rtmp), **RWd)

    g.arena_reset(keep=keep)
    Wout = carve([128, 12, D], BF16)
    B_Wout = Buf("Wout")
    for j in range(3):
        k.dma(Wout[:, 4 * j:4 * j + 4, :], g.wout_b[512 * j:512 * (j + 1), :].rearrange("(kt p) n -> p kt n", p=128),
              reads=[g.wb_bufs["wout"]], writes=[B_Wout])
    Hbs = carve([128, NCH, D], BF16)
    B_Hbs = Buf("Hbs")
    Hst = [carve([128, D]) for _ in range(2)]
    B_Hst = [Buf("Hst0"), Buf("Hst1")]
    Hbf = carve([128, D], BF16)
    B_Hbf = Buf("Hbf")
    xs = [carve([128, D], BF16) for _ in range(2)]
    btm = [carve([128, 512], BF16) for _ in range(2)]
    bfm = [carve([128, 4, 128], BF16) for _ in range(2)]
    cfm = [carve([128, 4, 128], BF16) for _ in range(2)]
    zs = [carve([128, D], BF16) for _ in range(2)]
    xres = [carve([128, D]) for _ in range(2)]
    B_ld = [{n: Buf(n + str(i)) for n in ("xs", "btm", "bfm", "cfm", "zs", "xres")} for i in range(2)]
    Atm = carve([128, NCH, 32])
    tmpe = carve([128, NCH, 32])
    wq = carve([128, NCH, 32])
    dlast = carve([128, NCH, 32])
    B_sm = Buf("small")
    A2E = [carve([128, NCH, 128], BF16) for _ in range(2)]
    nA2 = [carve([128, NCH, 128], BF16) for _ in range(2)]
    B_A3 = [Buf("A30"), Buf("A31")]
    rt1 = carve([128, 512])
    Gm = [carve([128, 4, 128], BF16) for _ in range(2)]
    B_Gm = Buf("Gm")
    Lb = [carve([128, 4, 128], BF16) for _ in range(2)]
    B_Lb = [Buf("Lb0"), Buf("Lb1")]
    GL = [carve([128, 16, 128], BF16) for _ in range(2)]
    Cd = [carve([128, 16, 128], BF16) for _ in range(2)]
    B_GL = [Buf("GL0"), Buf("GL1")]
    B_Cd = [Buf("Cd0"), Buf("Cd1")]
    xdt = [carve([128, D], BF16) for _ in range(2)]
    xw = [carve([128, D], BF16) for _ in range(2)]
    B_xdt = [Buf("xdt0"), Buf("xdt1")]
    B_xw = [Buf("xw0"), Buf("xw1")]
    y1 = carve([128, D])
    h1 = y1
    ynb = carve([128, D], BF16)
    hn2b = ynb
    ysfm = carve([128, 8, 128], BF16)
    hn2f = carve([128, 8, 128], BF16)
    stat = carve([128, 4])
    stat2 = carve([128, 4])
    junk = carve([128, D], BF16)
    B_y1, B_ynb, B_ysfm, B_hn2f = (Buf(n) for n in ("y1", "ynb", "ysfm", "hn2f"))
    B_y2, B_h1, B_hn2b = B_y1, B_y1, B_ynb
    B_stat, B_stat2, B_junk = Buf("stat"), Buf("stat2"), Buf("junkb")

    h3 = lambda a: a.rearrange("p (h e) -> p h e", h=16)
    hb = lambda a: a.unsqueeze(2).to_broadcast([128, 16, 64])

    def mmA():
        ins = None
        for c in range(NCH):
            for kk in range(3):
                ins = T_.matmul(pst[1][:, c * 32:c * 32 + 16], lhsT=g.maskF_b[:], rhs=dsp[kk][:, c, 0:16],
                                start=(kk == 0), stop=(kk == 2))
            for kk in range(3):
                ins = T_.matmul(pst[1][:, c * 32 + 16:c * 32 + 32], lhsT=g.maskB_b[:], rhs=dsp[kk][:, c, 16:32],
                                start=(kk == 0), stop=(kk == 2))
            for kk in range(3):
                ins = T_.matmul(pst[0][:, c * 32:(c + 1) * 32], lhsT=g.ones_b[:], rhs=dsp[kk][:, c, :],
                                start=(kk == 0), stop=(kk == 2))
        return ins
    k.op("pe", mmA, reads=[B_dt, BC], writes=[PB[0], PB[1]])
    fl = lambda a: a.rearrange("p c h -> p (c h)")
    rws = dict(reads=[PB[0], PB[1], B_sm, B_dt], writes=[B_sm])
    k.op("dve", lambda: V.tensor_copy(out=fl(Atm), in_=pst[1][:, :]), **rws)
    k.op("dve", lambda: V.tensor_tensor(out=fl(tmpe), in0=pst[0][:, :], in1=fl(Atm), op=ALU.subtract), **rws)
    k.op("act", lambda: A_.activation(out=fl(tmpe), in_=fl(tmpe), func=AF.Exp), **rws)
    k.op("dve", lambda: V.tensor_tensor(out=fl(wq), in0=fl(tmpe), in1=fl(dt_t), op=ALU.mult), **rws)
    k.op("act", lambda: A_.activation(out=fl(dlast), in_=pst[0][:, :], func=AF.Exp), **rws)
    for d, mk in enumerate((g.maskF_b, g.maskB_b)):
        def mmF():
            ins = None
            for c in range(NCH):
                for kk in range(3):
                    ins = T_.matmul(pst[2 + c // 4][0:32, (c % 4) * 128:(c % 4 + 1) * 128], lhsT=dsp[kk][:, c, :],
                                    rhs=mk[:], start=(kk == 0), stop=(kk == 2))
            return ins
        k.op("pe", mmF, reads=[B_dt, BC], writes=[PB[2], PB[3], PB[4], PB[5]])
        for j in range(4):
            src = pst[2 + j][0:32, :]
            cs = slice(4 * j, 4 * j + 4)
            v3 = lambda a: a.rearrange("p c q -> p (c q)")
            rw3 = dict(reads=[PB[2 + j], B_A3[d]], writes=[B_A3[d]])
            k.op("dve", lambda: V.tensor_copy(out=v3(A2E[d][0:32, cs, :]), in_=src), **rw3)
            k.op("act", lambda: A_.activation(out=v3(A2E[d][64:96, cs, :]), in_=src, func=AF.Exp), **rw3)
            k.op("dve", lambda: V.tensor_tensor(out=rt1[0:32, :], in0=src, in1=v3(A2E[d][0:32, cs, :]),
                                                op=ALU.subtract), **rw3)
            k.op("dve", lambda: V.tensor_copy(out=v3(A2E[d][32:64, cs, :]), in_=rt1[0:32, :]), **rw3)
        rw3 = dict(reads=[B_A3[d]], writes=[B_A3[d]])
        k.op("dve", lambda: V.tensor_scalar(out=nA2[d][0:64].rearrange("p c q -> p (c q)"),
                                            in0=A2E[d][0:64].rearrange("p c q -> p (c q)"), scalar1=-1.0,
                                            scalar2=None, op0=ALU.mult), **rw3)

    def load_chunk(c, i, full):
        tmc = slice(c * 128, (c + 1) * 128)
        k.dma(xs[i], g.xs_d[tmc, :], reads=[BD["xs"]], writes=[B_ld[i]["xs"]])
        k.dma(btm[i], g.btm_d[tmc, :], reads=[BD["btm"]], writes=[B_ld[i]["btm"]])
        if full:
            k.dma(bfm[i], g.bfm_d.rearrange("(gq n) t -> n gq t", n=128)[:, :, tmc], reads=[BD["bfm"]],
                  writes=[B_ld[i]["bfm"]])
            k.dma(cfm[i], g.cfm_d.rearrange("(gq n) t -> n gq t", n=128)[:, :, tmc], reads=[BD["cfm"]],
                  writes=[B_ld[i]["cfm"]])
            k.dma(zs[i], g.zs_d[tmc, :], reads=[BD["zs"]], writes=[B_ld[i]["zs"]])
            k.dma(xres[i], g.x[b, tmc, :], writes=[B_ld[i]["xres"]])

    def state_update(d, i, c):
        def mm():
            ins = None
            for gq in range(4):
                ins = T_.matmul(pst[6 + gq // 2][:, (gq % 2) * 256:(gq % 2 + 1) * 256],
                                lhsT=btm[i][:, gq * 128:(gq + 1) * 128], rhs=xw[d][:, gq * 256:(gq + 1) * 256],
                                start=True, stop=True)
            return ins
        k.op("pe", mm, reads=[B_ld[i]["btm"], B_xw[d]], writes=[PB[6], PB[7]])
        k.op("dve", lambda: V.tensor_tensor(out=h3(Hst[d]), in0=h3(Hst[d]), in1=hb(dlast[:, c, 16 * d:16 * d + 16]),
                                            op=ALU.mult), reads=[B_Hst[d], B_sm], writes=[B_Hst[d]])
        for half in range(2):
            sl = slice(half * 512, (half + 1) * 512)
            k.op("dve", lambda: V.tensor_tensor(out=Hst[d][:, sl], in0=Hst[d][:, sl], in1=pst[6 + half][:, :],
                                                op=ALU.add), reads=[B_Hst[d], PB[6 + half]], writes=[B_Hst[d]])

    for d in range(2):
        k.op("dve", lambda: V.memset(Hst[d], 0.0), writes=[B_Hst[d]])
    for n_, c in enumerate(range(NCH - 1, -1, -1)):
        i = n_ % 2
        load_chunk(c, i, False)
        k.op("dve", lambda: V.tensor_copy(out=Hbs[:, c, :], in_=Hst[1]), reads=[B_Hst[1]], writes=[B_Hbs])
        k.op("dve", lambda: V.tensor_tensor(out=h3(xw[1]), in0=h3(xs[i]), in1=hb(wq[:, c, 16:32]), op=ALU.mult),
             reads=[B_ld[i]["xs"], B_sm], writes=[B_xw[1]])
        state_update(1, i, c)
    def core(c):
        i = c % 2
        load_chunk(c, i, True)
        yield

        def mmg():
            ins = None
            for gq in range(4):
                ins = T_.matmul(pst[0][:, gq * 128:(gq + 1) * 128], lhsT=bfm[i][:, gq, :], rhs=cfm[i][:, gq, :],
                                start=True, stop=True)
            return ins
        k.op("pe", mmg, reads=[B_ld[i]["bfm"], B_ld[i]["cfm"]], writes=[PB[0]])
        yield
        for d, mk in enumerate((g.maskF_f, g.maskB_f)):
            k.op("dve", lambda: V.tensor_tensor(out=Gm[d], in0=pst[0][:, :].rearrange("p (a q) -> p a q", a=4),
                                                in1=mk[:].unsqueeze(1).to_broadcast([128, 4, 128]), op=ALU.mult),
                 reads=[PB[0], BC], writes=[B_Gm])
            yield
            k.op("dve", lambda: V.tensor_tensor(out=h3(xdt[d]), in0=h3(xs[i]),
                                                in1=hb(dt_t[:, c, 16 * d:16 * d + 16]), op=ALU.mult),
                 reads=[B_ld[i]["xs"], B_dt], writes=[B_xdt[d]])
            yield
        k.op("dve", lambda: V.tensor_tensor(out=h3(xw[0]), in0=h3(xs[i]), in1=hb(wq[:, c, 0:16]), op=ALU.mult),
             reads=[B_ld[i]["xs"], B_sm], writes=[B_xw[0]])
        yield
        k.op("dve", lambda: V.tensor_copy(out=Hbf, in_=Hst[0]), reads=[B_Hst[0]], writes=[B_Hbf])
        yield
        nb_ = 0
        for d in range(2):
            for grp in range(4):
                pl, pd = (2, 6) if nb_ % 2 == 0 else (3, 7)
                lbi = nb_ % 2
                nb_ += 1

                def mmb():
                    ins = None
                    for i4 in range(4):
                        hh = d * 16 + grp * 4 + i4
                        es_ = g.esel_b[0:64, hh * 128:(hh + 1) * 128]
                        o = pst[pl][:, i4 * 128:(i4 + 1) * 128]
                        T_.matmul(o, lhsT=es_, rhs=A2E[d][0:64, c, :], start=True, stop=False)
                        T_.matmul(o, lhsT=nA2[d][0:64, c, :], rhs=es_, start=False, stop=True)
                        ins = T_.matmul(pst[pd][:, i4 * 128:(i4 + 1) * 128],
                                        lhsT=g.esel_b[64:96, hh * 128:(hh + 1) * 128], rhs=A2E[d][64:96, c, :],
                                        start=True, stop=True)
                    return ins
                k.op("pe", mmb, reads=[B_A3[d], BC], writes=[PB[pl], PB[pd]])
                yield
                k.op("act", lambda: A_.activation(out=Lb[lbi], in_=pst[pl][:, :].rearrange("p (a q) -> p a q", a=4),
                                                  func=AF.Exp), reads=[PB[pl]], writes=[B_Lb[lbi]])
                yield
                h0 = grp * 4
                k.op("dve", lambda: V.scalar_tensor_tensor(
                    out=GL[d][:, h0:h0 + 4, :], in0=Lb[lbi], scalar=1.0,
                    in1=Gm[d][:, grp, :].unsqueeze(1).to_broadcast([128, 4, 128]), op0=ALU.min, op1=ALU.mult),
                    reads=[B_Lb[lbi], B_Gm], writes=[B_GL[d]])
                yield
                k.op("dve", lambda: V.tensor_tensor(
                    out=Cd[d][:, h0:h0 + 4, :], in0=pst[pd][:, :].rearrange("p (a q) -> p a q", a=4),
                    in1=cfm[i][:, grp, :].unsqueeze(1).to_broadcast([128, 4, 128]), op=ALU.mult),
                    reads=[PB[pd], B_ld[i]["cfm"]], writes=[B_Cd[d]])
                yield

        def mmy():
            ins = None
            for h in range(16):
                o = pst[4 + h // 8][:, (h % 8) * 64:(h % 8 + 1) * 64]
                hs = slice(h * 64, (h + 1) * 64)
                T_.matmul(o, lhsT=GL[0][:, h, :], rhs=xdt[0][:, hs], start=True, stop=False)
                T_.matmul(o, lhsT=GL[1][:, h, :], rhs=xdt[1][:, hs], start=False, stop=False)
                T_.matmul(o, lhsT=Cd[0][:, h, :], rhs=Hbf[:, hs], start=False, stop=False)
                ins = T_.matmul(o, lhsT=Cd[1][:, h, :], rhs=Hbs[:, c, hs], start=False, stop=True)
            return ins
        k.op("pe", mmy, reads=B_GL + B_Cd + B_xdt + [B_Hbf, B_Hbs], writes=[PB[4], PB[5]])
        yield
        state_update(0, i, c)
        yield

    def epi(c):
        i = c % 2
        k.op("dve", lambda: V.tensor_tensor(out=h3(y1), in0=h3(xs[i]), in1=hb(g.dsk_bc[:]), op=ALU.mult),
             reads=[B_ld[i]["xs"], BC], writes=[B_y1])
        yield
        for half in range(2):
            sl = slice(half * 512, (half + 1) * 512)
            k.op("dve", lambda: V.tensor_tensor(out=y1[:, sl], in0=y1[:, sl], in1=pst[4 + half][:, :], op=ALU.add),
                 reads=[B_y1, PB[4 + half]], writes=[B_y1])
            yield
        k.op("dve", lambda: V.tensor_tensor(out=y1, in0=y1, in1=zs[i], op=ALU.mult), reads=[B_y1, B_ld[i]["zs"]],
             writes=[B_y2])
        yield
        _rmsnorm_stats(g, y1, stat, junk, [B_y2], B_stat, B_junk, 1.0 / D)
        yield
        k.op("dve", lambda: V.tensor_scalar(out=ynb, in0=y1, scalar1=stat[:, 2:3], scalar2=None, op0=ALU.mult),
             reads=[B_y2, B_stat], writes=[B_ynb])
        yield
        psb = pst[1][:, :].bitcast(BF16)

        def tr():
            ins = None
            for kt in range(8):
                ins = T_.transpose(out=psb[:, kt * 128:(kt + 1) * 128], in_=ynb[:, kt * 128:(kt + 1) * 128],
                                   identity=g.ident_b[:])
            return ins
        k.op("pe", tr, reads=[B_ynb, BC], writes=[PB[1]])
        yield
        k.op("dve", lambda: V.tensor_tensor(out=ysfm, in0=psb.rearrange("p (k t) -> p k t", k=8),
                                            in1=g.nssd_fm[:].unsqueeze(2).to_broadcast([128, 8, 128]), op=ALU.mult),
             reads=[PB[1], BC], writes=[B_ysfm])
        yield

        for half in range(2):
            ns = slice(half * 512, (half + 1) * 512)

            def mmo():
                ins = None
                for kt in range(8):
                    T_.matmul(pst[1][:, :], lhsT=ysfm[:, kt, :], rhs=Wout[:, kt, ns], start=(kt == 0), stop=False)
                for kt in range(4):
                    ins = T_.matmul(pst[1][:, :], lhsT=g.ys5_fm[:, kt, c * 128:(c + 1) * 128],
                                    rhs=Wout[:, 8 + kt, ns], start=False, stop=(kt == 3))
                return ins
            k.op("pe", mmo, reads=[B_ysfm, B_Wout, g.B_ys5], writes=[PB[1]])
            yield
            k.op("dve", lambda: V.tensor_tensor(out=h1[:, ns], in0=pst[1][:, :], in1=xres[i][:, ns], op=ALU.add),
                 reads=[PB[1], B_ld[i]["xres"]], writes=[B_h1])
            yield
        k.dma(g.h1_d[c * 128:(c + 1) * 128, :], h1, reads=[B_h1], writes=[BD["h1"]])
        yield
        if b == 0 and "d_h1" in g.dbg_out:
            k.dma(g.dbg_out["d_h1"][c * 128:(c + 1) * 128, :], h1, reads=[B_h1], writes=[Buf("dbg")])
            yield
        _rmsnorm_stats(g, h1, stat2, junk, [B_h1], B_stat2, B_junk, 1.0 / D)
        yield
        k.op("dve", lambda: V.tensor_scalar(out=hn2b, in0=h1, scalar1=stat2[:, 2:3], scalar2=None, op0=ALU.mult),
             reads=[B_h1, B_stat2], writes=[B_hn2b])
        yield
        psb3 = pst[1][:, :].bitcast(BF16)

        def tr2():
            ins = None
            for kt in range(8):
                ins = T_.transpose(out=psb3[:, kt * 128:(kt + 1) * 128], in_=hn2b[:, kt * 128:(kt + 1) * 128],
                                   identity=g.ident_b[:])
            return ins
        k.op("pe", tr2, reads=[B_hn2b, BC], writes=[PB[1]])
        yield
        k.op("dve", lambda: V.tensor_tensor(out=hn2f, in0=psb3.rearrange("p (k t) -> p k t", k=8),
                                            in1=g.nffn_fm[:].unsqueeze(2).to_broadcast([128, 8, 128]), op=ALU.mult),
             reads=[PB[1], BC], writes=[B_hn2f])
        yield
        k.dma(g.hn2fm_d.rearrange("(kt p) t -> p kt t", p=128)[:, :, c * 128:(c + 1) * 128], hn2f, reads=[B_hn2f],
              writes=[BD["hn2"]])
        yield

    def zipgen(gens):
        gens = [x for x in gens if x is not None]
        while gens:
            for x in list(gens):
                try:
                    next(x)
                except StopIteration:
                    gens.remove(x)

    import os
    for t in range(NCH + 1):
        ge = epi(t - 1) if t >= 1 else None
        gc = core(t) if t < NCH else None
        if os.environ.get("NOPIPE") or (os.environ.get("PIPE") is not None and "ssd" not in os.environ["PIPE"].split(",")):
            zipgen([ge])
            zipgen([gc])
        else:
            if ge is not None:
                for _ in range(4):
                    next(ge)
            zipgen([ge, gc])


def phase_d(g, b):
    nc, k, P = g.nc, g.k, g.P
    V, A_, T_ = nc.vector, nc.scalar, nc.tensor
    carve, pst, PB, BC = g.carve, g.pst, g.PB, g.B_const
    BD = g.B_dram
    g.arena_reset()
    hn2 = carve([128, 8, L], BF16)
    B_hn2 = Buf("hn2sb")
    for kt in range(8):
        k.dma(hn2[:, kt, :], g.hn2fm_d[kt * 128:(kt + 1) * 128, :], reads=[BD["hn2"]], writes=[B_hn2])
    wv = [carve([128, 8, 128], BF16) for _ in range(2)]
    wg = [carve([128, 8, 128], BF16) for _ in range(2)]
    B_w = [Buf("wvg0"), Buf("wvg1")]
    prev = [carve([128, L + 2]) for _ in range(2)]
    preg = [carve([128, L + 2]) for _ in range(2)]
    B_prev = [Buf("prev0"), Buf("prev1")]
    B_preg = [Buf("preg0"), Buf("preg1")]
    accv = [carve([128, L]) for _ in range(2)]
    accg = [carve([128, L]) for _ in range(2)]
    B_accv, B_accg = [Buf("accv0"), Buf("accv1")], [Buf("accg0"), Buf("accg1")]
    gt = [carve([128, L], BF16) for _ in range(2)]
    B_gt = [Buf("gt0"), Buf("gt1")]
    for i in range(2):
        for t_, bb in ((prev[i], B_prev[i]), (preg[i], B_preg[i])):
            k.op("dve", lambda: V.memset(t_[:, 0:1], 0.0), writes=[bb])
            k.op("dve", lambda: V.memset(t_[:, L + 1:L + 2], 0.0), writes=[bb])
    cw, cb = g.cw_ffn, g.cb_ffn
    NF = DFF // 128

    def fwload(m):
        i = m % 2
        k.dma(wv[i], g.wup_b[:, m * 128:(m + 1) * 128].rearrange("(kt p) n -> p kt n", p=128),
              reads=[g.wb_bufs["wup"]], writes=[B_w[i]])
        k.dma(wg[i], g.wup_b[:, DFF + m * 128:DFF + (m + 1) * 128].rearrange("(kt p) n -> p kt n", p=128),
              reads=[g.wb_bufs["wup"]], writes=[B_w[i]])

    def fs1(m):
        i = m % 2
        if m == 0:
            fwload(0)
        if m + 1 < NF:
            fwload(m + 1)
        for nt in range(4):
            for (wtile, dstt, bdst, pi) in ((wv[i], prev[i], B_prev[i], 2 + nt % 2),
                                            (wg[i], preg[i], B_preg[i], 4 + nt % 2)):
                def mm():
                    ins = None
                    for kt in range(8):
                        ins = T_.matmul(pst[pi][:, :], lhsT=wtile[:, kt, :], rhs=hn2[:, kt, nt * 512:(nt + 1) * 512],
                                        start=(kt == 0), stop=(kt == 7))
                    return ins
                k.op("pe", mm, reads=[B_w[i], B_hn2], writes=[PB[pi]])
                k.op("act", lambda: A_.copy(out=dstt[:, 1 + nt * 512:1 + (nt + 1) * 512], in_=pst[pi][:, :]),
                     reads=[PB[pi]], writes=[bdst])

    def fs2(m):
        i = m % 2
        for (src, bsrc, a_, ba, ci) in ((prev[i], B_prev[i], accv[i], B_accv[i], m),
                                        (preg[i], B_preg[i], accg[i], B_accg[i], NF + m)):
            k.op("act", lambda: A_.activation(out=a_, in_=src[:, 1:L + 1], func=AF.Identity, scale=cw[:, ci, 1:2],
                                              bias=cb[:, ci:ci + 1]), reads=[bsrc, BC], writes=[ba])
            for tap in (0, 2):
                k.op("dve", lambda: V.scalar_tensor_tensor(out=a_, in0=src[:, tap:tap + L],
                                                           scalar=cw[:, ci, tap:tap + 1], in1=a_, op0=ALU.mult,
                                                           op1=ALU.add), reads=[bsrc, ba, BC], writes=[ba])

    def fs3(m):
        i = m % 2
        k.op("act", lambda: A_.activation(out=accg[i], in_=accg[i], func=AF.Silu), reads=[B_accg[i]],
             writes=[B_accg[i]])
        k.op("dve", lambda: V.tensor_tensor(out=gt[i], in0=accg[i], in1=accv[i], op=ALU.mult),
             reads=[B_accg[i], B_accv[i]], writes=[B_gt[i]])
        k.dma(g.g_d[m * 128:(m + 1) * 128, :], gt[i], reads=[B_gt[i]], writes=[BD["g"]])

    pipeline(NF, [(0, fs1), (1, fs2), (1, fs3)], "ffn")
    g.arena_reset()
    Wdn = carve([128, NF, D], BF16)
    B_Wdn = Buf("Wdn")
    for j in range(0, NF, 2):
        k.dma(Wdn[:, j:j + 2, :], g.wdn_b[j * 128:(j + 2) * 128, :].rearrange("(kt p) n -> p kt n", p=128),
              reads=[g.wb_bufs["wdn"]], writes=[B_Wdn])
    gmt = [carve([128, NF, 512], BF16) for _ in range(2)]
    B_gmt = [Buf("gmt0"), Buf("gmt1")]
    h1t = [carve([128, D]) for _ in range(2)]
    B_h1t = [Buf("h1t0"), Buf("h1t1")]
    o1 = [carve([128, D]) for _ in range(2)]
    o2 = [carve([128, D]) for _ in range(2)]
    B_o1 = [Buf("o10"), Buf("o11")]
    B_o2 = [Buf("o20"), Buf("o21")]
    stat = [carve([128, 4]) for _ in range(2)]
    B_stat = [Buf("fst0"), Buf("fst1")]
    junk = carve([128, D], BF16)
    B_junk = Buf("fjunk")
    for nt in range(4):
        gi_ = nt % 2
        k.dma(gmt[gi_], g.g_d.rearrange("(kt p) t -> p kt t", p=128)[:, :, nt * 512:(nt + 1) * 512], reads=[BD["g"]],
              writes=[B_gmt[gi_]])
        for cc in range(4):
            c = nt * 4 + cc
            i = c % 2
            k.dma(h1t[i], g.h1_d[c * 128:(c + 1) * 128, :], reads=[BD["h1"]], writes=[B_h1t[i]])
            pb0 = 6 if i == 0 else 0

            def mm():
                ins = None
                for half in range(2):
                    for kt in range(NF):
                        ins = T_.matmul(pst[pb0 + half][:, :], lhsT=gmt[gi_][:, kt, cc * 128:(cc + 1) * 128],
                                        rhs=Wdn[:, kt, half * 512:(half + 1) * 512], start=(kt == 0),
                                        stop=(kt == NF - 1))
                return ins
            k.op("pe", mm, reads=[B_gmt[gi_], B_Wdn], writes=[PB[pb0], PB[pb0 + 1]])
            for half in range(2):
                sl = slice(half * 512, (half + 1) * 512)
                k.op("dve", lambda: V.tensor_tensor(out=o1[i][:, sl], in0=pst[pb0 + half][:, :], in1=h1t[i][:, sl],
                                                    op=ALU.add), reads=[PB[pb0 + half], B_h1t[i]], writes=[B_o1[i]])
            _rmsnorm_stats(g, o1[i], stat[i], junk, [B_o1[i]], B_stat[i], B_junk, 1.0 / D)
            k.op("dve", lambda: V.scalar_tensor_tensor(out=o2[i], in0=o1[i], scalar=stat[i][:, 2:3], in1=g.nfin_bc[:],
                                                       op0=ALU.mult, op1=ALU.mult),
                 reads=[B_o1[i], B_stat[i], BC], writes=[B_o2[i]])
            k.dma(g.out[b, c * 128:(c + 1) * 128, :], o2[i], reads=[B_o2[i]], writes=[Buf("outd")])


def prep_inputs(inputs, S):
    per_core = []
    hc = host_consts()
    xs = np.ascontiguousarray(inputs["x"], dtype=np.float32)
    ncore = xs.shape[0] // S
    for ci in range(ncore):
        m = {"x": xs[ci * S:(ci + 1) * S]}
        for n, v in inputs.items():
            if n == "x":
                continue
            a = np.ascontiguousarray(v, dtype=np.float32)
            if n != "norm_final_w":
                a = a[0]
            m[n] = np.ascontiguousarray(a)
        m.update(hc)
        per_core.append(m)
    return per_core


def kernel(**inputs):
    S = inputs["x"].shape[0] // NCORES
    nc = build_program(S)
    in_maps = prep_inputs(inputs, S)
    res = run_bass_kernel_spmd(nc, in_maps, core_ids=list(range(NCORES)))
    return np.concatenate([r["out"] for r in res.results], axis=0).astype(np.float32)
```
